# Optimizing a Trainium2 kernel written in Bass

```python
import math
import jax, jax.numpy as jnp
from jax import lax
import numpy as np

D_MODEL = 2048
BATCH = 4
SEQ = 2048
DEPTH = 2
DEC_BATCH = 128
DEC_SEQ = 8
PAST_LEN = 2048
PAGE_SIZE = 128

LRU_WIDTH = D_MODEL // 4
LRU_BLOCKS = 8
LRU_BLOCK = LRU_WIDTH // LRU_BLOCKS
LRU_C = 8.0
CONV_W = 4
RET_HEADS = 8
RET_DIM = D_MODEL // 4 // RET_HEADS
RET_WIDTH = RET_HEADS * RET_DIM
MLSTM_HEADS = 4
MLSTM_DIM = D_MODEL // 4 // MLSTM_HEADS
MLSTM_WIDTH = MLSTM_HEADS * MLSTM_DIM
DIL_GROUPS = ((128, 1), (512, 4), (2048, 16))
DIL_SPAN = 128
DIL_HEADS_PER_GROUP = 4
ATT_DIM = 64
N_DIL_HEADS = len(DIL_GROUPS) * DIL_HEADS_PER_GROUP
DIL_WIDTH = N_DIL_HEADS * ATT_DIM
DIL_OUT = DIL_HEADS_PER_GROUP * ATT_DIM
ATT_SCALE = ATT_DIM ** -0.5
ROPE_THETA = 10000.0
N_BRANCHES = 4
D_FF = 4 * D_MODEL
CHUNK = 128
LN_EPS = 1e-5
GN_EPS = 1e-6
ALPHA = (2 * DEPTH) ** 0.25
BETA = (8 * DEPTH) ** -0.25
_IN_SEGMENTS = (LRU_WIDTH, LRU_WIDTH,
                RET_WIDTH, RET_WIDTH, RET_WIDTH, RET_WIDTH,
                MLSTM_WIDTH, MLSTM_WIDTH, MLSTM_WIDTH,
                DIL_WIDTH, DIL_WIDTH, DIL_WIDTH,
                N_BRANCHES * D_MODEL)
N_IN_COLS = sum(_IN_SEGMENTS)

kernel_name = 'hybrid_lru_ret_mlstm_dilswa_decode_step'


def _layer_norm(x, g, b):
    xf = x.astype(jnp.float32)
    mu = jnp.mean(xf, axis=-1, keepdims=True)
    var = jnp.mean(jnp.square(xf - mu), axis=-1, keepdims=True)
    return ((xf - mu) * lax.rsqrt(var + LN_EPS)).astype(x.dtype) * g + b


def _head_norm(h, g):
    N, T = h.shape[:2]
    hf = h.astype(jnp.float32)
    mu = jnp.mean(hf, axis=-1, keepdims=True)
    var = jnp.mean(jnp.square(hf - mu), axis=-1, keepdims=True)
    return ((hf - mu) * lax.rsqrt(var + GN_EPS)).reshape(N, T, -1).astype(g.dtype) * g


def _rotary(x, pos):
    half = x.shape[-1] // 2
    inv = ROPE_THETA ** (-jnp.arange(half, dtype=jnp.float32) / half)
    ang = pos.astype(jnp.float32)[:, None] * inv[None, :]
    cos = jnp.cos(ang)[None, :, None, :]
    sin = jnp.sin(ang)[None, :, None, :]
    xf = x.astype(jnp.float32)
    x1, x2 = xf[..., :half], xf[..., half:]
    return jnp.concatenate([x1 * cos - x2 * sin, x2 * cos + x1 * sin], axis=-1).astype(x.dtype)


def _causal_conv(x, buf, w, b):
    T = x.shape[1]
    xp = jnp.concatenate([buf.astype(x.dtype), x], axis=1)
    y = xp[:, 0:T] * w[0]
    for j in range(1, CONV_W):
        y = y + xp[:, j:j + T] * w[j]
    return y + b, xp[:, xp.shape[1] - (CONV_W - 1):]


def _chunks(x, chunk):
    N, T = x.shape[:2]
    return x.reshape((N, T // chunk, chunk) + x.shape[2:]).swapaxes(0, 1)


def _unchunk(x):
    x = x.swapaxes(0, 1)
    return x.reshape((x.shape[0], x.shape[1] * x.shape[2]) + x.shape[3:])


def _rglru(x, h0, wa, ba, wx, bx, lam):
    N, T, R = x.shape
    f32 = jnp.float32
    xb = x.reshape(N, T, LRU_BLOCKS, LRU_BLOCK)
    r = jax.nn.sigmoid((jnp.einsum('ntbi,bij->ntbj', xb, wa).reshape(N, T, R) + ba).astype(f32))
    i = jax.nn.sigmoid((jnp.einsum('ntbi,bij->ntbj', xb, wx).reshape(N, T, R) + bx).astype(f32))
    log_a = -LRU_C * r * jax.nn.softplus(-lam.astype(f32))
    a = jnp.exp(log_a)
    bt = jnp.sqrt(-jnp.expm1(2.0 * log_a)) * (i * x.astype(f32))
    bt = bt.at[:, 0].add(a[:, 0] * h0.astype(f32))

    def combine(left, right):
        a_l, b_l = left
        a_r, b_r = right
        return a_l * a_r, a_r * b_l + b_r

    _, h = lax.associative_scan(combine, (a, bt), axis=1)
    return h.astype(x.dtype), h[:, -1]


def _retention(q, k, v, s0, chunk):
    f32 = jnp.float32
    H, dh = q.shape[2], q.shape[3]
    q, v = q.astype(f32), v.astype(f32)
    k = k.astype(f32) * dh ** -0.5
    log_g = jnp.log1p(-jnp.exp2(-5.0 - jnp.arange(H, dtype=f32)))
    t = jnp.arange(chunk, dtype=f32)
    rel = t[:, None] - t[None, :]
    d_in = jnp.where(rel >= 0, jnp.exp(jnp.maximum(rel, 0.0) * log_g[:, None, None]), 0.0)
    d_q = jnp.exp((t[:, None] + 1.0) * log_g[None, :])
    d_k = jnp.exp((chunk - 1.0 - t[:, None]) * log_g[None, :])
    d_s = jnp.exp(chunk * log_g)

    def step(S, blk):
        qb, kb, vb = blk
        sc = jnp.einsum('nlhd,nmhd->nhlm', qb, kb) * d_in
        o = (jnp.einsum('nhlm,nmhe->nlhe', sc, vb)
             + jnp.einsum('nlhd,nhde->nlhe', qb, S) * d_q[None, :, :, None])
        S = S * d_s[:, None, None] + jnp.einsum('nlhd,nlhe->nhde', kb * d_k[None, :, :, None], vb)
        return S, o

    S, o = lax.scan(step, s0.astype(f32), (_chunks(q, chunk), _chunks(k, chunk), _chunks(v, chunk)))
    return _unchunk(o), S


def _mlstm(q, k, v, ig, fg, c0, n0, m0, chunk):
    f32 = jnp.float32
    dh = q.shape[-1]
    q, v = q.astype(f32), v.astype(f32)
    k = k.astype(f32) * dh ** -0.5
    ig = ig.astype(f32)
    lf = jax.nn.log_sigmoid(fg.astype(f32))
    causal = jnp.tril(jnp.ones((chunk, chunk), dtype=bool))

    def step(carry, blk):
        C, n, m = carry
        qb, kb, vb, ib, lfb = blk
        F = jnp.cumsum(lfb, axis=1)
        logD = F[:, :, None, :] - F[:, None, :, :] + ib[:, None, :, :]
        logD = jnp.where(causal[None, :, :, None], logD, -jnp.inf)
        log_inter = F + m[:, None, :]
        m_t = jnp.maximum(jnp.max(logD, axis=2), log_inter)
        Dm = jnp.exp(logD - m_t[:, :, None, :])
        inter = jnp.exp(log_inter - m_t)
        s = jnp.einsum('nlhd,nmhd->nlmh', qb, kb) * Dm
        num = (jnp.einsum('nlmh,nmhe->nlhe', s, vb)
               + inter[..., None] * jnp.einsum('nlhd,nhde->nlhe', qb, C))
        den = jnp.sum(s, axis=2) + inter * jnp.einsum('nlhd,nhd->nlh', qb, n)
        h = num / jnp.maximum(jnp.abs(den), jnp.exp(-m_t))[..., None]
        m_new = m_t[:, -1]
        wk = jnp.exp(F[:, -1:, :] - F + ib - m_new[:, None, :])
        dec = jnp.exp(F[:, -1] + m - m_new)
        C = dec[..., None, None] * C + jnp.einsum('nlhd,nlhe->nhde', kb * wk[..., None], vb)
        n = dec[..., None] * n + jnp.einsum('nlhd,nlh->nhd', kb, wk)
        return (C, n, m_new), h

    carry0 = (c0.astype(f32), n0.astype(f32), m0.astype(f32))
    xs = (_chunks(q, chunk), _chunks(k, chunk), _chunks(v, chunk), _chunks(ig, chunk), _chunks(lf, chunk))
    (C, n, m), h = lax.scan(step, carry0, xs)
    return _unchunk(h), C, n, m


def _banded_attention(q, k, v):
    G, M, H, dh = q.shape
    blk = DIL_SPAN
    nb = -(-M // blk)
    padw = ((0, 0), (0, nb * blk - M), (0, 0), (0, 0))
    qb = jnp.pad(q, padw).reshape(G, nb, blk, H, dh)
    kb = jnp.pad(k, padw).reshape(G, nb, blk, H, dh)
    vb = jnp.pad(v, padw).reshape(G, nb, blk, H, dh)

    def with_prev(t):
        prev = jnp.concatenate([jnp.zeros_like(t[:, :1]), t[:, :-1]], axis=1)
        return jnp.concatenate([prev, t], axis=2)

    kk, vv = with_prev(kb), with_prev(vb)
    qi = jnp.arange(blk)[:, None] + blk
    kj = jnp.arange(2 * blk)[None, :]
    rel = qi - kj
    band = (rel >= 0) & (rel <= DIL_SPAN)
    has_prev = (jnp.arange(nb)[:, None, None] > 0) | (kj[None] >= blk)
    mask = band[None] & has_prev
    s = jnp.einsum('gbqhd,gbkhd->gbhqk', qb, kk).astype(jnp.float32) * ATT_SCALE
    s = jnp.where(mask[None, :, None], s, -jnp.inf)
    lse = jax.nn.logsumexp(s, axis=-1)
    p = jnp.exp(s - lse[..., None]).astype(v.dtype)
    o = jnp.einsum('gbhqk,gbkhd->gbqhd', p, vv).reshape(G, nb * blk, H, dh)[:, :M]
    lse = lse.transpose(0, 1, 3, 2).reshape(G, nb * blk, H)[:, :M]
    return o, lse


def _dilated_prompt(q, k, v, dil):
    N, T, H, dh = q.shape
    M = T // dil

    def to_res(t):
        return t.reshape(N, M, dil, H, dh).swapaxes(1, 2).reshape(N * dil, M, H, dh)

    o, lse = _banded_attention(to_res(q), to_res(k), to_res(v))
    o = o.reshape(N, dil, M, H, dh).swapaxes(1, 2).reshape(N, T, H, dh)
    lse = lse.reshape(N, dil, M, H).swapaxes(1, 2).reshape(N, T, H)
    return o, lse


def _dilated_sample(q, k, v, buf, dil):
    S = q.shape[1]
    lb = buf.shape[1]
    k_all = jnp.concatenate([buf[:, :, 0].astype(k.dtype), k], axis=1)
    v_all = jnp.concatenate([buf[:, :, 1].astype(v.dtype), v], axis=1)
    idx = lb + jnp.arange(S)[:, None] - dil * jnp.arange(DIL_SPAN + 1)[None, :]
    valid = idx >= 0
    idx = jnp.maximum(idx, 0)
    kg = jnp.take(k_all, idx, axis=1)
    vg = jnp.take(v_all, idx, axis=1)
    s = jnp.einsum('nshd,nsjhd->nhsj', q, kg).astype(jnp.float32) * ATT_SCALE
    s = jnp.where(valid, s, -jnp.inf)
    lse = jax.nn.logsumexp(s, axis=-1)
    p = jnp.exp(s - lse[..., None]).astype(v.dtype)
    o = jnp.einsum('nhsj,nsjhd->nshd', p, vg)
    return o, lse.transpose(0, 2, 1)


def _mixer(u, p, st, pos, dil_bufs):
    N, T, _ = u.shape
    dt = u.dtype
    chunk = min(CHUNK, T)
    splits = [int(s) for s in np.cumsum(_IN_SEGMENTS)[:-1]]
    (a_x, a_gate, b_q, b_k, b_v, b_g, c_qk, c_v, c_z,
     d_q, d_k, d_v, gates) = jnp.split(u @ p['w_in'], splits, axis=-1)

    a_c, lru_conv = _causal_conv(a_x, st['lru_conv'], p['lru_conv_w'], p['lru_conv_b'])
    a_h, lru_h = _rglru(a_c, st['lru_h'], p['lru_wa'], p['lru_ba'], p['lru_wx'], p['lru_bx'], p['lru_lambda'])
    y_a = a_h * jax.nn.gelu(a_gate)

    rs = (N, T, RET_HEADS, RET_DIM)
    ret, ret_s = _retention(_rotary(b_q.reshape(rs), pos), _rotary(b_k.reshape(rs), pos),
                            b_v.reshape(rs), st['ret'], chunk)
    y_b = _head_norm(ret, p['ret_gn_g']) * jax.nn.silu(b_g)

    ms = (N, T, MLSTM_HEADS, MLSTM_DIM)
    c_conv, mlstm_conv = _causal_conv(c_qk, st['mlstm_conv'], p['mlstm_conv_w'], p['mlstm_conv_b'])
    c_act = jax.nn.silu(c_conv)
    cq = jnp.einsum('nthd,hde->nthe', c_act.reshape(ms), p['mlstm_wq'])
    ck = jnp.einsum('nthd,hde->nthe', c_act.reshape(ms), p['mlstm_wk'])
    gate_in = jnp.concatenate([cq.reshape(N, T, -1), ck.reshape(N, T, -1), c_v], axis=-1)
    gpre = gate_in @ p['mlstm_w_gates'] + p['mlstm_b_gates']
    h, mc, mn, mm = _mlstm(cq, ck, c_v.reshape(ms), gpre[..., :MLSTM_HEADS], gpre[..., MLSTM_HEADS:],
                           st['mlstm_c'], st['mlstm_n'], st['mlstm_m'], chunk)
    y_c = jax.nn.sigmoid(c_z) * (_head_norm(h, p['mlstm_gn_g']) + p['mlstm_skip'] * c_act)

    ds = (N, T, N_DIL_HEADS, ATT_DIM)
    dq = _rotary(d_q.reshape(ds), pos)
    dk = _rotary(d_k.reshape(ds), pos)
    dv = d_v.reshape(ds)
    outs, lses, kv_rows = [], [], []
    for g, (win, dil) in enumerate(DIL_GROUPS):
        hs = slice(g * DIL_HEADS_PER_GROUP, (g + 1) * DIL_HEADS_PER_GROUP)
        qg, kg, vg = dq[:, :, hs], dk[:, :, hs], dv[:, :, hs]
        rows = jnp.stack([kg, vg], axis=2)
        if dil_bufs is None:
            o, l = _dilated_prompt(qg, kg, vg, dil)
            rows = rows[:, T - min(win, T):]
        else:
            o, l = _dilated_sample(qg, kg, vg, dil_bufs[g], dil)
        outs.append(o)
        lses.append(l)
        kv_rows.append(rows)
    wts = jax.nn.softmax(jnp.stack(lses), axis=0)
    y_d = jnp.einsum('gnth,gnthd->nthd', wts.astype(dt), jnp.stack(outs)).reshape(N, T, DIL_OUT)

    g_a, g_b, g_c, g_d = jnp.split(jax.nn.sigmoid(gates), N_BRANCHES, axis=-1)
    merged = (g_a * (y_a @ p['w_br_a']) + g_b * (y_b @ p['w_br_b'])
              + g_c * (y_c @ p['w_br_c']) + g_d * (y_d @ p['w_br_d']))
    mix = merged @ p['w_out']
    new = dict(lru_conv=lru_conv, lru_h=lru_h.astype(dt), ret=ret_s.astype(dt),
               mlstm_conv=mlstm_conv, mlstm_c=mc.astype(dt), mlstm_n=mn.astype(dt),
               mlstm_m=mm.astype(dt), dil1_kv=kv_rows[0], dil4_kv=kv_rows[1], dil16_kv=kv_rows[2])
    return mix, new


def _layer(x, p, st, pos, dil_bufs):
    mix, new = _mixer(x, p, st, pos, dil_bufs)
    x = _layer_norm(ALPHA * x + mix, p['ln1_g'], p['ln1_b'])
    hid = jnp.square(jax.nn.relu(x @ p['w_up'] + p['b_up']))
    x = _layer_norm(ALPHA * x + hid @ p['w_down'] + p['b_down'], p['ln2_g'], p['ln2_b'])
    return x, new


def _zero_state(n, dtype):
    return dict(lru_conv=jnp.zeros((n, CONV_W - 1, LRU_WIDTH), dtype),
                lru_h=jnp.zeros((n, LRU_WIDTH), dtype),
                ret=jnp.zeros((n, RET_HEADS, RET_DIM, RET_DIM), dtype),
                mlstm_conv=jnp.zeros((n, CONV_W - 1, MLSTM_WIDTH), dtype),
                mlstm_c=jnp.zeros((n, MLSTM_HEADS, MLSTM_DIM, MLSTM_DIM), dtype),
                mlstm_n=jnp.zeros((n, MLSTM_HEADS, MLSTM_DIM), dtype),
                mlstm_m=jnp.zeros((n, MLSTM_HEADS), dtype))


def _stack(states, name):
    return jnp.stack([s[name] for s in states])


def setup_inputs(seed: int = 0) -> dict:
    key = jax.random.key(seed)
    ks = iter(jax.random.split(key, 64))
    f32 = jnp.float32

    def nrm(shape, scale):
        return jax.random.normal(next(ks), shape, f32) * scale

    L, D = DEPTH, D_MODEL
    Hm, dm = MLSTM_HEADS, MLSTM_DIM
    Hd = DIL_HEADS_PER_GROUP
    wl = [min(w, PAST_LEN) for w, _ in DIL_GROUPS]
    a_c = jax.random.uniform(next(ks), (L, LRU_WIDTH), f32, 0.9, 0.999)
    sig = a_c ** (1.0 / LRU_C)
    lru_lambda = jnp.log(sig) - jnp.log1p(-sig)
    b_f = jnp.broadcast_to(jnp.linspace(3.0, 6.0, Hm, dtype=f32), (L, Hm))
    mlstm_b_gates = jnp.concatenate([nrm((L, Hm), 0.1), b_f + nrm((L, Hm), 0.01)], axis=-1)
    return {
        'x_prompt': nrm((BATCH, SEQ, D), 1.0),
        'x_sample': nrm((DEC_BATCH, DEC_SEQ, D), 1.0),
        'state_lru_conv': nrm((L, DEC_BATCH, CONV_W - 1, LRU_WIDTH), 1.0),
        'state_lru_h': nrm((L, DEC_BATCH, LRU_WIDTH), 0.5),
        'state_ret': nrm((L, DEC_BATCH, RET_HEADS, RET_DIM, RET_DIM), 0.5),
        'state_mlstm_conv': nrm((L, DEC_BATCH, CONV_W - 1, MLSTM_WIDTH), 1.0),
        'state_mlstm_c': nrm((L, DEC_BATCH, Hm, dm, dm), 0.5),
        'state_mlstm_n': nrm((L, DEC_BATCH, Hm, dm), 0.5),
        'state_mlstm_m': nrm((L, DEC_BATCH, Hm), 1.0),
        'cache_dil1_kv': nrm((L, DEC_BATCH, wl[0], 2, Hd, ATT_DIM), 1.0),
        'cache_dil4_kv': nrm((L, DEC_BATCH, wl[1], 2, Hd, ATT_DIM), 1.0),
        'cache_dil16_kv': nrm((L, DEC_BATCH, wl[2], 2, Hd, ATT_DIM), 1.0),
        'w_in': nrm((L, D, N_IN_COLS), D ** -0.5),
        'lru_conv_w': nrm((L, CONV_W, LRU_WIDTH), CONV_W ** -0.5),
        'lru_conv_b': nrm((L, LRU_WIDTH), 0.01),
        'lru_wa': nrm((L, LRU_BLOCKS, LRU_BLOCK, LRU_BLOCK), LRU_BLOCK ** -0.5),
        'lru_ba': nrm((L, LRU_WIDTH), 0.01),
        'lru_wx': nrm((L, LRU_BLOCKS, LRU_BLOCK, LRU_BLOCK), LRU_BLOCK ** -0.5),
        'lru_bx': nrm((L, LRU_WIDTH), 0.01),
        'lru_lambda': lru_lambda,
        'ret_gn_g': 1.0 + nrm((L, RET_WIDTH), 0.02),
        'mlstm_conv_w': nrm((L, CONV_W, MLSTM_WIDTH), CONV_W ** -0.5),
        'mlstm_conv_b': nrm((L, MLSTM_WIDTH), 0.01),
        'mlstm_wq': nrm((L, Hm, dm, dm), dm ** -0.5),
        'mlstm_wk': nrm((L, Hm, dm, dm), dm ** -0.5),
        'mlstm_w_gates': nrm((L, 3 * MLSTM_WIDTH, 2 * Hm), (3 * MLSTM_WIDTH) ** -0.5),
        'mlstm_b_gates': mlstm_b_gates,
        'mlstm_skip': 1.0 + nrm((L, MLSTM_WIDTH), 0.02),
        'mlstm_gn_g': 1.0 + nrm((L, MLSTM_WIDTH), 0.02),
        'w_br_a': nrm((L, LRU_WIDTH, D), LRU_WIDTH ** -0.5),
        'w_br_b': nrm((L, RET_WIDTH, D), RET_WIDTH ** -0.5),
        'w_br_c': nrm((L, MLSTM_WIDTH, D), MLSTM_WIDTH ** -0.5),
        'w_br_d': nrm((L, DIL_OUT, D), DIL_OUT ** -0.5),
        'w_out': nrm((L, D, D), BETA * D ** -0.5),
        'ln1_g': 1.0 + nrm((L, D), 0.02),
        'ln1_b': nrm((L, D), 0.02),
        'w_up': nrm((L, D, D_FF), D ** -0.5),
        'b_up': nrm((L, D_FF), 0.01),
        'w_down': nrm((L, D_FF, D), BETA * D_FF ** -0.5),
        'b_down': nrm((L, D), 0.01),
        'ln2_g': 1.0 + nrm((L, D), 0.02),
        'ln2_b': nrm((L, D), 0.02),
    }


def reference(x_prompt, x_sample, state_lru_conv, state_lru_h, state_ret, state_mlstm_conv,
              state_mlstm_c, state_mlstm_n, state_mlstm_m, cache_dil1_kv, cache_dil4_kv, cache_dil16_kv,
              w_in, lru_conv_w, lru_conv_b, lru_wa, lru_ba, lru_wx, lru_bx, lru_lambda, ret_gn_g,
              mlstm_conv_w, mlstm_conv_b, mlstm_wq, mlstm_wk, mlstm_w_gates, mlstm_b_gates, mlstm_skip,
              mlstm_gn_g, w_br_a, w_br_b, w_br_c, w_br_d, w_out, ln1_g, ln1_b, w_up, b_up, w_down, b_down,
              ln2_g, ln2_b):
    pos_p = jnp.arange(x_prompt.shape[1], dtype=jnp.int32)
    pos_s = PAST_LEN + jnp.arange(x_sample.shape[1], dtype=jnp.int32)
    y_p, y_s = x_prompt, x_sample
    new_p, new_s = [], []
    for l in range(DEPTH):
        p = dict(w_in=w_in[l], lru_conv_w=lru_conv_w[l], lru_conv_b=lru_conv_b[l], lru_wa=lru_wa[l],
                 lru_ba=lru_ba[l], lru_wx=lru_wx[l], lru_bx=lru_bx[l], lru_lambda=lru_lambda[l],
                 ret_gn_g=ret_gn_g[l], mlstm_conv_w=mlstm_conv_w[l], mlstm_conv_b=mlstm_conv_b[l],
                 mlstm_wq=mlstm_wq[l], mlstm_wk=mlstm_wk[l], mlstm_w_gates=mlstm_w_gates[l],
                 mlstm_b_gates=mlstm_b_gates[l], mlstm_skip=mlstm_skip[l], mlstm_gn_g=mlstm_gn_g[l],
                 w_br_a=w_br_a[l], w_br_b=w_br_b[l], w_br_c=w_br_c[l], w_br_d=w_br_d[l], w_out=w_out[l],
                 ln1_g=ln1_g[l], ln1_b=ln1_b[l], w_up=w_up[l], b_up=b_up[l], w_down=w_down[l],
                 b_down=b_down[l], ln2_g=ln2_g[l], ln2_b=ln2_b[l])
        y_p, ns_p = _layer(y_p, p, _zero_state(x_prompt.shape[0], x_prompt.dtype), pos_p, None)
        new_p.append(ns_p)
        st = dict(lru_conv=state_lru_conv[l], lru_h=state_lru_h[l], ret=state_ret[l],
                  mlstm_conv=state_mlstm_conv[l], mlstm_c=state_mlstm_c[l], mlstm_n=state_mlstm_n[l],
                  mlstm_m=state_mlstm_m[l])
        y_s, ns_s = _layer(y_s, p, st, pos_s, (cache_dil1_kv[l], cache_dil4_kv[l], cache_dil16_kv[l]))
        new_s.append(ns_s)
    p_lru_conv, s_lru_conv = _stack(new_p, 'lru_conv'), _stack(new_s, 'lru_conv')
    p_lru_h, s_lru_h = _stack(new_p, 'lru_h'), _stack(new_s, 'lru_h')
    p_ret, s_ret = _stack(new_p, 'ret'), _stack(new_s, 'ret')
    p_mconv, s_mconv = _stack(new_p, 'mlstm_conv'), _stack(new_s, 'mlstm_conv')
    p_mc, s_mc = _stack(new_p, 'mlstm_c'), _stack(new_s, 'mlstm_c')
    p_mn, s_mn = _stack(new_p, 'mlstm_n'), _stack(new_s, 'mlstm_n')
    p_mm, s_mm = _stack(new_p, 'mlstm_m'), _stack(new_s, 'mlstm_m')
    p_kv1, s_kv1 = _stack(new_p, 'dil1_kv'), _stack(new_s, 'dil1_kv')
    p_kv4, s_kv4 = _stack(new_p, 'dil4_kv'), _stack(new_s, 'dil4_kv')
    p_kv16, s_kv16 = _stack(new_p, 'dil16_kv'), _stack(new_s, 'dil16_kv')
    return (y_p, y_s,
            p_lru_conv, p_lru_h, p_ret, p_mconv, p_mc, p_mn, p_mm, p_kv1, p_kv4, p_kv16,
            s_lru_conv, s_lru_h, s_ret, s_mconv, s_mc, s_mn, s_mm, s_kv1, s_kv4, s_kv16)
```

```python
import math
from contextlib import ExitStack
import numpy as np
import concourse.bass as bass
import concourse.mybir as mybir
from concourse.bass_utils import run_bass_kernel_spmd

F32 = mybir.dt.float32
BF = mybir.dt.bfloat16
AF = mybir.ActivationFunctionType
ALU = mybir.AluOpType
AXL = mybir.AxisListType

D = 2048
DFF = 8192
NL = 2
NIN = 15104
T_P = 2048
N_S = 128
NSEQ = 16
NTOK = T_P + N_S
ALPHA = (2 * NL) ** 0.25
LN_EPS = 1e-5
GN_EPS = 1e-6
SLABW = 256
NSLOT = 24
EPOCH = 3500
ALIGNW = 32
_STOP = None
_DBG = False


class _Stop(Exception):
    pass


def _ckpt(name):
    if _STOP is not None and name == _STOP:
        raise _Stop()


class Buf:
    __slots__ = ("name", "w", "rc", "rd")

    def __init__(self, name=""):
        self.name = name
        self.w = None
        self.rc = {}
        self.rd = []


class Sched:
    ENG = ("pe", "act", "dve", "pool", "sp")

    def __init__(self):
        self.ops = {e: [] for e in self.ENG}
        self.rr = {"pool": 0, "sp": 0}
        self.cnt = {}
        self.slot_ev = {}
        self.store_evs = []

    def op(self, eng, fn, reads=(), writes=(), dma=False, extra=()):
        deps = list(extra)
        for b in reads:
            if b.w is not None:
                deps.append(b.w)
        for b in writes:
            if b.w is not None:
                deps.append(b.w)
            deps.extend(b.rc.values())
            deps.extend(b.rd)
        idx = len(self.ops[eng])
        slot = None
        if dma:
            k = self.rr[eng]
            self.rr[eng] = (k + 1) % NSLOT
            slot = (eng, k)
            prev = self.slot_ev.get(slot)
            if prev is not None:
                deps.append(prev)
            self.cnt[slot] = self.cnt.get(slot, 0) + 1
            ev = ("d", eng, k, 16 * self.cnt[slot])
            self.slot_ev[slot] = ev
        else:
            ev = ("c", eng, idx)
        d2 = set()
        for d in deps:
            if d[0] == "c" and d[1] == eng and eng == "pe":
                continue
            d2.add(d)
        for d in d2:
            if d[0] == "c":
                self.ops[d[1]][d[2]]["sig"] = True
        self.ops[eng].append(dict(fn=fn, deps=d2, sig=False, slot=slot))
        for b in writes:
            b.w = ev
            b.rc = {}
            b.rd = []
        for b in reads:
            if b in writes:
                continue
            if ev[0] == "c":
                b.rc[eng] = ev
            else:
                b.rd.append(ev)
        return ev

    def barrier(self):
        last = []
        for e in self.ENG:
            if self.ops[e]:
                last.append(("c", e, len(self.ops[e]) - 1))
        dma_evs = list(self.slot_ev.values())
        for e in self.ENG:
            self.op(e, None, extra=[x for x in last if x[1] != e] + dma_evs)

    def emit(self, eng, h, csem, dsem):
        rank = {}
        for e in self.ENG:
            r = 0
            lst = []
            for o in self.ops[e]:
                if o["sig"]:
                    r += 1
                lst.append(r)
            rank[e] = lst
        seen_c = {e: 0 for e in self.ENG}
        seen_d = {}
        myrank = 0
        for o in self.ops[eng]:
            for d in sorted(o["deps"]):
                if d[0] == "c":
                    need = rank[d[1]][d[2]]
                    if seen_c[d[1]] < need:
                        h.wait_ge(csem[d[1]][(need - 1) // EPOCH], (need - 1) % EPOCH + 1)
                        seen_c[d[1]] = need
                else:
                    key = (d[1], d[2])
                    if seen_d.get(key, 0) < d[3]:
                        h.wait_ge(dsem[key], d[3])
                        seen_d[key] = d[3]
            ins = o["fn"](h) if o["fn"] is not None else None
            if o["slot"] is not None:
                ins.then_inc(dsem[o["slot"]], 16)
                if o["sig"]:
                    h.nop().then_inc(csem[eng][myrank // EPOCH], 1)
                    myrank += 1
            elif o["sig"]:
                if ins is None:
                    ins = h.nop()
                ins.then_inc(csem[eng][myrank // EPOCH], 1)
                myrank += 1


def _consts():
    c = {}
    f32 = np.float32
    c["identf"] = np.eye(128, dtype=f32)
    perm = np.zeros((128, 128), f32)
    for i in range(128):
        d = i % 64
        p = i + 32 if d < 32 else i - 32
        perm[p, i] = 1.0
    c["perm"] = perm
    c["ones"] = np.ones((128, 128), f32)
    sel = np.zeros((4, 4, 128), f32)
    for h in range(4):
        sel[h, h, :] = 1.0
    c["sel"] = sel
    half = 32
    inv = (10000.0 ** (-np.arange(half, dtype=f32) / half)).astype(f32)
    pos = np.concatenate([np.arange(T_P), np.tile(2048 + np.arange(8), NSEQ)]).astype(f32)
    ang = (pos[None, :] * inv[:, None]).astype(f32)
    cos = np.cos(ang).astype(f32)
    sin = np.sin(ang).astype(f32)
    cosT = np.zeros((128, NTOK), f32)
    sinT = np.zeros((128, NTOK), f32)
    for p in range(128):
        d = p % 64
        cosT[p] = cos[d % 32]
        sinT[p] = -sin[d % 32] if d < 32 else sin[d % 32]
    c["rot"] = np.stack([cosT, sinT], axis=1)
    H = 8
    log_g = np.log1p(-np.exp2(-5.0 - np.arange(H, dtype=np.float64)))
    for v, (L, nseq) in enumerate(((128, 1), (8, 16))):
        t = np.arange(128) % L
        sq = np.arange(128) // L
        din = np.zeros((128, H, 128), np.float64)
        for h in range(H):
            rel = t[None, :] - t[:, None]
            ok = (rel >= 0) & (sq[None, :] == sq[:, None])
            din[:, h, :] = np.where(ok, np.exp(np.maximum(rel, 0) * log_g[h]), 0.0) * 0.125
        c["din%d" % v] = din.astype(f32)
        dq = np.zeros((128, 4, 128), np.float64)
        for p in range(128):
            for hp in range(4):
                h = hp * 2 + p // 64
                dq[p, hp, :] = np.exp((t + 1.0) * log_g[h])
        c["dq%d" % v] = dq.astype(f32)
        dk = np.zeros((128, 512), np.float64)
        for h in range(H):
            dk[:, h * 64:(h + 1) * 64] = (np.exp((L - 1.0 - t) * log_g[h]) * 0.125)[:, None]
        c["dk%d" % v] = dk.astype(f32)
        ds = np.zeros((128, 4), np.float64)
        for p in range(128):
            for hp in range(4):
                ds[p, hp] = np.exp(L * log_g[hp * 2 + p // 64])
        c["ds%d" % v] = ds.astype(f32)
        cm = ((t[None, :] >= t[:, None]) & (sq[None, :] == sq[:, None])).astype(f32)
        c["cm%d" % v] = cm
    ms = np.zeros((128, NSEQ), f32)
    ms[np.arange(128), np.arange(128) // 8] = 1.0
    c["mseq"] = ms
    m = np.arange(128)[:, None]
    l = np.arange(128)[None, :]
    dm = np.zeros((10, 128, 128), f32)
    dm[0] = (m <= l)
    dm[1] = (m >= l)
    dm[2] = (m <= l) & ((l - m) % 4 == 0)
    dm[3] = ((l - m) % 4 == 0)
    dm[4] = (m >= l) & ((l - m) % 4 == 0)
    dm[5] = (m <= l) & ((l - m) % 16 == 0)
    dm[6] = ((l - m) % 16 == 0)
    same = (m // 8 == l // 8)
    dm[7] = same & (l - m >= 0)
    dm[8] = same & ((l - m == 0) | (l - m == 4))
    dm[9] = same & (l - m == 0)
    c["dmask"] = np.ascontiguousarray(dm.transpose(1, 0, 2))
    c0 = np.zeros((128, 4, 8), f32)
    for s in range(8):
        c0[s:, :, s] = 1.0
    c1 = np.zeros((128, 4, 4, 8), f32)
    for r in range(4):
        c1[:, r, :, r] = 1.0
        c1[1:, r, :, r + 4] = 1.0
    c2 = np.zeros((128, 8, 4, 8), f32)
    for r in range(8):
        c2[:, r, :, r] = 1.0
    c["cmask0"] = c0
    c["cmask1"] = c1
    c["cmask2"] = c2
    return c


def build_program():
    nc = bass.Bass("TRN2", target_bir_lowering=False)
    S = Sched()
    cst = _consts()

    def din(name, shape):
        return nc.dram_tensor(name, list(shape), F32, kind="ExternalInput").ap()

    def dout(name, shape):
        return nc.dram_tensor(name, list(shape), F32, kind="ExternalOutput").ap()

    def dint(name, shape):
        return nc.dram_tensor(name, list(shape), F32, kind="Internal").ap()

    I = {}
    I["xT"] = din("xT", [128, 16, NTOK])
    for nm, shp in (("w_in", [NL, D, NIN]), ("w_br_a", [NL, 512, D]), ("w_br_b", [NL, 512, D]),
                    ("w_br_c", [NL, 512, D]), ("w_br_d", [NL, 256, D]), ("w_out", [NL, D, D]),
                    ("w_up", [NL, D, DFF]), ("w_down", [NL, DFF, D])):
        I[nm] = din(nm, shp)
    NPP = 256
    I["pp"] = din("pp", [NL, 128, NPP])
    I["lw"] = din("lw", [NL, 128, 8, 128])
    I["mw"] = din("mw", [NL, 128, 8, 128])
    I["wg"] = din("wg", [NL, 128, 12, 8])
    I["s_lconv"] = din("s_lconv", [NL, 128, 4, NSEQ, 3])
    I["s_lh"] = din("s_lh", [NL, 128, 4, NSEQ])
    I["s_ret"] = din("s_ret", [NL, 128, NSEQ, 4, 64])
    I["s_mconv"] = din("s_mconv", [NL, 128, 4, NSEQ, 3])
    I["s_mc"] = din("s_mc", [NL, 128, NSEQ, 4, 128])
    I["s_mn"] = din("s_mn", [NL, 128, NSEQ, 4])
    I["s_mm"] = din("s_mm", [NL, 4, NSEQ])
    I["c1"] = din("c1", [NL, NSEQ, 128, 512])
    I["c4"] = din("c4", [NL, NSEQ, 512, 512])
    I["c16"] = din("c16", [NL, NSEQ, 2048, 512])
    CI = {k: din("k_" + k, v.shape) for k, v in cst.items()}

    O = {}
    O["yT"] = dout("yT", [128, 16, NTOK])
    O["kT"] = dout("o_kT", [NL, 128, 6, NTOK])
    O["v"] = dout("o_v", [NL, NTOK, 768])
    for sfx, ns in (("p", 1), ("s", NSEQ)):
        O["lconv_" + sfx] = dout("o_lconv_" + sfx, [NL, 128, 4, ns, 3])
        O["lh_" + sfx] = dout("o_lh_" + sfx, [NL, 128, 4, ns])
        O["ret_" + sfx] = dout("o_ret_" + sfx, [NL, 128, ns, 4, 64])
        O["mconv_" + sfx] = dout("o_mconv_" + sfx, [NL, 128, 4, ns, 3])
        O["mc_" + sfx] = dout("o_mc_" + sfx, [NL, 128, ns, 4, 128])
        O["mn_" + sfx] = dout("o_mn_" + sfx, [NL, 128, ns, 4])
        O["mm_" + sfx] = dout("o_mm_" + sfx, [NL, 4, ns])
    DBG = dout("dbg", [128, 68, 512]) if _DBG else None
    XL1 = dint("xl1", [128, 16, NTOK])
    X1S = dint("x1s", [128, 16, 512])
    XL1b = [Buf("xl1_%d" % i) for i in range(5)]
    X1Sb = [Buf("x1s_%d" % i) for i in range(16)]

    es = ExitStack()
    ARW = 53200
    arena = es.enter_context(nc.sbuf_tensor("arena", [128, ARW], F32))
    PSF = [es.enter_context(nc.psum_tensor("psf%d" % i, [128, 512], F32)) for i in range(7)]
    PSB = es.enter_context(nc.psum_tensor("psb", [128, 1024], BF))
    PSFb = [Buf("psf%d" % i) for i in range(7)]
    PSBb = Buf("psb")

    class Arena:
        def __init__(self):
            self.off = 0

        def alloc(self, shape, dt=F32):
            n = int(np.prod(shape))
            words = n if dt == F32 else (n + 1) // 2
            self.off = (self.off + ALIGNW - 1) // ALIGNW * ALIGNW
            assert self.off + words <= ARW, ("arena overflow", self.off, words)
            ap = arena[:, self.off:self.off + words]
            self.off += words
            self.peak = max(getattr(self, "peak", 0), self.off)
            if dt != F32:
                ap = ap.bitcast(dt)
                if n % 2:
                    ap = ap[:, 0:n]
            if len(shape) == 2:
                ap = ap.rearrange("p (a b) -> p a b", a=shape[0])
            elif len(shape) == 3:
                ap = ap.rearrange("p (a b c) -> p a b c", a=shape[0], b=shape[1])
            return ap

    AR = Arena()

    def dve(fn, r, w):
        S.op("dve", fn, r, w)

    def act(fn, r, w):
        S.op("act", fn, r, w)

    def pe(fn, r, w):
        S.op("pe", fn, r, w)

    def dma(q, out, in_, r, w):
        return S.op(q, lambda e: e.dma_start(out=out, in_=in_), r, w, dma=True)

    def store(out, in_, r):
        ev = S.op("sp", lambda e: e.dma_start(out=out, in_=in_), r, (), dma=True)
        S.store_evs.append(ev)

    def TS(o, i, s1, s2, op0, op1=None):
        if op1 is None:
            return lambda e: e.tensor_scalar(out=o, in0=i, scalar1=s1, scalar2=None, op0=op0)
        return lambda e: e.tensor_scalar(out=o, in0=i, scalar1=s1, scalar2=s2, op0=op0, op1=op1)

    def TT(o, a, b, op):
        return lambda e: e.tensor_tensor(out=o, in0=a, in1=b, op=op)

    def STT(o, a, s, b, op0, op1):
        return lambda e: e.scalar_tensor_tensor(out=o, in0=a, scalar=s, in1=b, op0=op0, op1=op1)

    def ACT(o, i, f, bias=None, scale=None):
        kw = {}
        if bias is not None:
            kw["bias"] = bias
        if scale is not None:
            kw["scale"] = scale
        return lambda e: e.activation(out=o, in_=i, func=f, **kw)

    def CP(o, i):
        return lambda e: e.tensor_copy(out=o, in_=i)

    def ACP(o, i):
        return lambda e: e.activation(out=o, in_=i, func=AF.Copy)

    def MM(o, l_, r_, st=True, sp=True):
        return lambda e: e.matmul(o, l_, r_, start=st, stop=sp)

    def TRN(o, i, idn):
        return lambda e: e.transpose(o, i, idn)

    def SEQ(thunks):
        thunks = list(thunks)

        def f(e):
            ins = None
            for t in thunks:
                ins = t(e)
            return ins
        return f

    def SCAN(o, d0, d1, init, op0, op1):
        return lambda e: e.tensor_tensor_scan(out=o, data0=d0, data1=d1, initial=init, op0=op0, op1=op1)

    def MSET(ap, v):
        return lambda e: e.memset(ap, v)

    def RCP(o, i):
        return lambda e: e.reciprocal(out=o, in_=i)

    def RSUM(o, i):
        return lambda e: e.reduce_sum(out=o, in_=i, axis=AXL.X)

    class PSrot:
        def __init__(self, idxs):
            self.idxs = idxs
            self.k = 0

        def get(self):
            i = self.idxs[self.k % len(self.idxs)]
            self.k += 1
            return PSF[i], PSFb[i]

    PS = PSrot([0, 1, 2, 3])

    SLAB = [AR.alloc([16, SLABW], BF) for _ in range(2)]
    SLABb = [Buf("slab0"), Buf("slab1")]
    identb = AR.alloc([128], BF)
    permb = AR.alloc([128], BF)
    onesf = AR.alloc([128])
    onesb = AR.alloc([128], BF)
    zerob = AR.alloc([128], BF)
    self4 = AR.alloc([4, 128], BF)
    DIN = AR.alloc([8, 128])
    DQ = AR.alloc([4, 128])
    DK = AR.alloc([512])
    DS = AR.alloc([4])
    CM = AR.alloc([128], BF)
    MSEQ = AR.alloc([NSEQ])
    DMASK = AR.alloc([10, 128], BF)
    CMASK0 = AR.alloc([4, 8], BF)
    CMASK1 = AR.alloc([4, 4, 8], BF)
    CMASK2 = AR.alloc([8, 4, 8], BF)
    PP = AR.alloc([NPP])
    LWb = AR.alloc([8, 128], BF)
    MWb = AR.alloc([8, 128], BF)
    WGb = AR.alloc([12, 8], BF)
    CA = AR.alloc([4])
    ROT = AR.alloc([2, 512])
    Kb = Buf("consts")
    Tb = Buf("tables")
    Pb = Buf("params")
    ROTb = Buf("rot")

    PC = {}
    _o = [0]
    for nm, n in (("lcw", 16), ("lcb", 4), ("lba", 4), ("lbx", 4), ("llam", 4), ("mcw", 16), ("mcb", 4),
                  ("mskip", 4), ("mgn", 4), ("rgn", 8), ("ln1g", 16), ("ln1b", 16), ("ln2g", 16), ("ln2b", 16),
                  ("bdown", 16), ("bup", 64), ("bg", 2)):
        PC[nm] = _o[0]
        _o[0] += n
    assert _o[0] <= NPP

    def P_(name, j=0, parts=128):
        c = PC[name] + j
        return PP[0:parts, c:c + 1]

    dma("pool", identb, CI["identf"], (), (Kb,))
    dma("pool", permb, CI["perm"], (), (Kb,))
    dma("sp", onesf, CI["ones"], (), (Kb,))
    dma("pool", onesb, CI["ones"], (), (Kb,))
    dve(MSET(zerob, 0.0), (), (Kb,))
    dma("pool", self4[0:4], CI["sel"], (), (Kb,))
    dma("sp", MSEQ, CI["mseq"], (), (Kb,))
    dma("pool", DMASK, CI["dmask"], (), (Kb,))
    dma("pool", CMASK0, CI["cmask0"], (), (Kb,))
    dma("pool", CMASK1, CI["cmask1"], (), (Kb,))
    dma("pool", CMASK2, CI["cmask2"], (), (Kb,))

    def load_tables(v):
        dma("sp", DIN, CI["din%d" % v], (), (Tb,))
        dma("sp", DQ, CI["dq%d" % v], (), (Tb,))
        dma("sp", DK, CI["dk%d" % v], (), (Tb,))
        dma("sp", DS, CI["ds%d" % v], (), (Tb,))
        dma("pool", CM, CI["cm%d" % v], (), (Tb,))

    def load_params(l):
        dma("sp", PP, I["pp"][l], (), (Pb,))
        dma("pool", LWb, I["lw"][l], (), (Pb,))
        dma("pool", MWb, I["mw"][l], (), (Pb,))
        dma("pool", WGb, I["wg"][l], (), (Pb,))
        lam = PP[:, PC["llam"]:PC["llam"] + 4]
        act(ACT(CA, lam, AF.Exp, scale=-1.0), (Pb,), (Pb,))
        act(ACT(CA, CA, AF.Ln, bias=1.0), (Pb,), (Pb,))
        dve(TS(CA, CA, -8.0, None, ALU.mult), (Pb,), (Pb,))

    class Slabs:
        def __init__(self):
            self.descs = []
            self.loaded = 0
            self.cur = 0

        def announce(self, descs):
            self.descs.extend(descs)

        def _load(self, i):
            src, kcs, width, prow = self.descs[i]
            sl = SLAB[i % 2]
            dma("pool", sl[0:prow, 0:kcs, 0:width], src.rearrange("(k p) c -> p k c", p=prow), (), (SLABb[i % 2],))

        def next(self):
            i = self.cur
            self.cur += 1
            while self.loaded < min(len(self.descs), i + 2):
                self._load(self.loaded)
                self.loaded += 1
            return SLAB[i % 2], SLABb[i % 2]

    SL_ = Slabs()
    mark_persist = AR.off
    RG = (5, 8, 16)

    class Tile:
        pass

    tiles = []
    for i in range(4):
        t = Tile()
        t.N, t.nch, t.nseq, t.SL, t.tok0, t.sample, t.idx = 512, 4, 1, 512, i * 512, False, i
        tiles.append(t)
    t = Tile()
    t.N, t.nch, t.nseq, t.SL, t.tok0, t.sample, t.idx = 128, 1, NSEQ, 8, T_P, True, 4
    tiles.append(t)

    PH = {}

    def alloc_phase(l, smp):
        nseq = NSEQ if smp else 1
        SLn = 8 if smp else 512
        S.barrier()
        AR.off = mark_persist
        load_tables(1 if smp else 0)
        P = {}
        P["AXH"] = AR.alloc([4, nseq, 3 + SLn])
        P["CQH"] = AR.alloc([4, nseq, 3 + SLn])
        P["HST"] = AR.alloc([4, nseq])
        P["SF"] = AR.alloc([nseq, 4, 64])
        P["SBb"] = AR.alloc([nseq, 4, 64], BF)
        P["CF"] = AR.alloc([nseq, 4, 128])
        P["CBb"] = AR.alloc([nseq, 4, 128], BF)
        P["NF"] = AR.alloc([nseq, 4])
        P["NBb"] = AR.alloc([nseq * 4, 128], BF)
        P["MST"] = AR.alloc([nseq])
        stb = Buf("state")
        P["stb"] = stb
        if not smp:
            P["KD"] = [AR.alloc([2, RG[g] * 128], BF) for g in range(3)]
            P["VD"] = [AR.alloc([RG[g], 256], BF) for g in range(3)]
            P["KDb"] = [[Buf() for _ in range(RG[g])] for g in range(3)]
            for k in ("AXH", "CQH", "HST", "SF", "CF", "NF", "MST", "SBb", "CBb", "NBb"):
                dve(MSET(P[k], 0.0), (), (stb,))
        else:
            dma("sp", P["AXH"][:, :, :, 0:3], I["s_lconv"][l], (), (stb,))
            dma("sp", P["CQH"][:, :, :, 0:3], I["s_mconv"][l], (), (stb,))
            dma("sp", P["HST"], I["s_lh"][l], (), (stb,))
            dma("sp", P["SF"], I["s_ret"][l], (), (stb,))
            dma("pool", P["SBb"], I["s_ret"][l], (), (stb,))
            dma("sp", P["CF"], I["s_mc"][l], (), (stb,))
            dma("pool", P["CBb"], I["s_mc"][l], (), (stb,))
            dma("sp", P["NF"], I["s_mn"][l], (), (stb,))
            dma("sp", P["MST"][0:4], I["s_mm"][l], (), (stb,))
            for n in range(nseq):
                for h in range(4):
                    dve(TS(P["NBb"][:, n * 4 + h, :], onesf, P["NF"][:, n, h:h + 1], None, ALU.mult), (stb, Kb), (stb,))
        P["mark_tile"] = AR.off
        PH.clear()
        PH.update(P)

    def do_tile(l, T):
        N, nch, nseq, SLn, tok0 = T.N, T.nch, T.nseq, T.SL, T.tok0
        smp = T.sample
        sfx = "s" if smp else "p"
        last_p = (T.idx == 3)
        xsrc = I["xT"] if l == 0 else XL1
        xdst = XL1 if l == 0 else O["yT"]
        AXH, CQH, HST, SF, SBb, CF, CBb, NF, NBb, MST, stb = (PH[k] for k in ("AXH", "CQH", "HST", "SF", "SBb", "CF", "CBb", "NF", "NBb", "MST", "stb"))
        if not smp:
            KD, VD, KDb = PH["KD"], PH["VD"], PH["KDb"]
        AR.off = PH["mark_tile"]
        S.barrier()
        tile_off = AR.off
        XTB = AR.alloc([16, N], BF)
        xtb = Buf("xtb")
        YA = AR.alloc([4, N], BF)
        YB = AR.alloc([8, N], BF)
        YC = AR.alloc([4, N], BF)
        YD = AR.alloc([4, N], BF)
        yb_ = Buf("y")
        mark_phase = AR.off
        dma("pool", XTB, xsrc[:, :, tok0:tok0 + N], (XL1b[T.idx],), (xtb,))
        dma("sp", ROT[:, :, 0:N], CI["rot"][:, :, tok0:tok0 + N], (), (ROTb,))
        _ckpt("T%d_%d" % (l, T.idx))
        COS = ROT[:, 0, 0:N]
        SIN = ROT[:, 1, 0:N]
        SLc = 128 // nseq

        wi = I["w_in"][l]
        order = [("a_g", 512, 512), ("a_x", 0, 512), ("b_q", 1024, 512), ("b_k", 1536, 512), ("b_v", 2048, 512),
                 ("b_g", 2560, 512), ("c_qk", 3072, 512), ("c_v", 3584, 512), ("d_q", 4608, 768),
                 ("d_k", 5376, 768), ("d_v", 6144, 768), ("c_z", 4096, 512)]
        descs = []
        for nm, c0, w in order:
            for j in range(w // SLABW):
                descs.append((wi[:, c0 + j * SLABW:c0 + (j + 1) * SLABW], 16, SLABW, 128))
        brs = [("w_br_a", 512, 128), ("w_br_b", 512, 64), ("w_br_c", 512, 128), ("w_br_d", 256, 64)]
        for og in range(D // SLABW):
            for m in range(4):
                gc = 6912 + m * 2048 + og * SLABW
                descs.append((wi[:, gc:gc + SLABW], 16, SLABW, 128))
                nm, rows, prow = brs[m]
                descs.append((I[nm][l][:, og * SLABW:(og + 1) * SLABW], rows // prow, SLABW, prow))
        for og in range(D // SLABW):
            descs.append((I["w_out"][l][:, og * SLABW:(og + 1) * SLABW], 16, SLABW, 128))
        for hg in range(DFF // SLABW):
            descs.append((I["w_up"][l][:, hg * SLABW:(hg + 1) * SLABW], 16, SLABW, 128))
        for og in range(D // SLABW):
            for kq in range(4):
                descs.append((I["w_down"][l][kq * 2048:(kq + 1) * 2048, og * SLABW:(og + 1) * SLABW], 16, SLABW, 128))
        SL_.announce(descs)

        def proj_fm(ps, psb, sl, slb, c0, M):
            pe(SEQ([MM(ps[0:M, 0:N], sl[:, kc, c0:c0 + M], XTB[:, kc, 0:N], kc == 0, kc == 15) for kc in range(16)]), (slb, xtb), (psb,))

        def proj_tm(ps, psb, sl, slb, blk, width):
            pe(SEQ([MM(ps[:, 0:width], XTB[:, kc, blk * 128:(blk + 1) * 128], sl[:, kc, 0:width], kc == 0, kc == 15) for kc in range(16)]), (slb, xtb), (psb,))

        def conv(XH, fc, wname, bname, out, rb, wb):
            dve(TS(out, XH[:, fc, :, 0:SLn], P_(wname, fc * 4 + 0), P_(bname, fc), ALU.mult, ALU.add), rb, wb)
            for j in range(1, 4):
                dve(STT(out, XH[:, fc, :, j:j + SLn], P_(wname, fc * 4 + j), out, ALU.mult, ALU.add), rb + wb, wb)

        def v3(ap):
            return ap.rearrange("p (n s) -> p n s", n=nseq)

        def rotary(ps, psb, outs, scr):
            QF, T1, T2, rb_, QH, QL = scr
            act(ACP(QF, ps[:, 0:N]), (psb,), (rb_,))
            dve(CP(QH, QF), (rb_,), (rb_,))
            dve(TT(QL, QF, QH, ALU.subtract), (rb_,), (rb_,))
            psw, pswb = PS.get()
            pe(SEQ([MM(psw[:, 0:N], permb, QH, True, False), MM(psw[:, 0:N], permb, QL, False, True)]), (rb_, Kb), (pswb,))
            dve(TT(T1, QF, COS, ALU.mult), (rb_, ROTb), (rb_,))
            dve(TT(T2, psw[:, 0:N], SIN, ALU.mult), (pswb, ROTb), (rb_,))
            for (o, ob) in outs:
                dve(TT(o, T1, T2, ALU.add), (rb_,), (ob,))

        AG = AR.alloc([4, N])
        C1 = AR.alloc([N])
        CB = AR.alloc([N], BF)
        R_ = AR.alloc([N])
        I_ = AR.alloc([N])
        A_ = AR.alloc([N])
        Q_ = AR.alloc([N])
        sa = Buf("scrA")
        for half in range(2):
            sl, slb = SL_.next()
            for j in range(2):
                fc = half * 2 + j
                ps, psb = PS.get()
                proj_fm(ps, psb, sl, slb, j * 128, 128)
                act(ACT(AG[:, fc, :], ps[:, 0:N], AF.Gelu), (psb,), (sa,))
        for half in range(2):
            sl, slb = SL_.next()
            for j in range(2):
                fc = half * 2 + j
                ps, psb = PS.get()
                proj_fm(ps, psb, sl, slb, j * 128, 128)
                act(ACP(AXH[:, fc, :, 3:3 + SLn], v3(ps[:, 0:N])), (psb,), (stb,))
                conv(AXH, fc, "lcw", "lcb", v3(C1), [stb, Pb], [sa])
                dve(CP(CB, C1), (sa,), (sa,))
                psr, psrb = PS.get()
                pe(MM(psr[:, 0:N], LWb[:, fc * 2, :], CB), (sa, Pb), (psrb,))
                psi, psib = PS.get()
                pe(MM(psi[:, 0:N], LWb[:, fc * 2 + 1, :], CB), (sa, Pb), (psib,))
                act(ACT(R_, psr[:, 0:N], AF.Sigmoid, bias=P_("lba", fc)), (psrb, Pb), (sa,))
                act(ACT(I_, psi[:, 0:N], AF.Sigmoid, bias=P_("lbx", fc)), (psib, Pb), (sa,))
                act(ACT(A_, R_, AF.Exp, scale=CA[:, fc:fc + 1]), (sa, Pb), (sa,))
                act(ACT(Q_, A_, AF.Square), (sa,), (sa,))
                act(ACT(Q_, Q_, AF.Sqrt, bias=1.0, scale=-1.0), (sa,), (sa,))
                dve(TT(I_, I_, C1, ALU.mult), (sa,), (sa,))
                dve(TT(I_, I_, Q_, ALU.mult), (sa,), (sa,))
                for n in range(nseq):
                    sl_n = slice(n * SLn, (n + 1) * SLn)
                    dve(SCAN(I_[:, sl_n], A_[:, sl_n], I_[:, sl_n], HST[:, fc, n:n + 1], ALU.mult, ALU.add), (sa, stb), (sa,))
                dve(CP(HST[:, fc, :], v3(I_)[:, :, SLn - 1]), (sa,), (stb,))
                dve(TT(YA[:, fc, :], I_, AG[:, fc, :], ALU.mult), (sa,), (yb_,))
        if smp or last_p:
            store(O["lconv_" + sfx][l], AXH[:, :, :, SLn:SLn + 3], (stb,))
            store(O["lh_" + sfx][l], HST, (stb,))
        else:
            dve(CP(AXH[:, :, :, 0:3], AXH[:, :, :, SLn:SLn + 3]), (stb,), (stb,))
        AR.off = mark_phase
        S.barrier()
        _ckpt("%s%d_%d" % ("A", l, T.idx))

        scrB = (AR.alloc([N]), AR.alloc([N]), AR.alloc([N]), Buf("rotB"), AR.alloc([N], BF), AR.alloc([N], BF))
        QB = AR.alloc([4, N], BF)
        QS = AR.alloc([4, N], BF)
        KB_ = AR.alloc([4, N], BF)
        KT = AR.alloc([nch, 512], BF)
        VB = AR.alloc([nch, 512], BF)
        VEX = AR.alloc([nseq, 512], BF) if smp else None
        GS = AR.alloc([8, N])
        OT = AR.alloc([8, N])
        SC = AR.alloc([4, 128], BF)
        sb_ = Buf("scrB")
        qr = AR.alloc([N])
        for half in range(2):
            sl, slb = SL_.next()
            for j in range(2):
                fc = half * 2 + j
                ps, psb = PS.get()
                proj_fm(ps, psb, sl, slb, j * 128, 128)
                rotary(ps, psb, [(qr, sb_)], scrB)
                dve(CP(QB[:, fc, :], qr), (sb_,), (sb_,))
                for c in range(nch):
                    dve(TT(QS[:, fc, c * 128:(c + 1) * 128], qr[:, c * 128:(c + 1) * 128], DQ[:, fc, :], ALU.mult), (sb_, Tb), (sb_,))
        _ckpt("B1_%d_%d" % (l, T.idx))
        for half in range(2):
            sl, slb = SL_.next()
            for j in range(2):
                fc = half * 2 + j
                ps, psb = PS.get()
                proj_fm(ps, psb, sl, slb, j * 128, 128)
                rotary(ps, psb, [(KB_[:, fc, :], sb_)], scrB)
        _ckpt("B2_%d_%d" % (l, T.idx))
        for c in range(nch):
            pe(SEQ([TRN(PSB[:, fc * 128:(fc + 1) * 128], KB_[:, fc, c * 128:(c + 1) * 128], identb) for fc in range(4)]), (sb_, Kb), (PSBb,))
            dve(TT(KT[:, c, :], PSB[:, 0:512], DK, ALU.mult), (PSBb, Tb), (sb_,))
        for half in range(2):
            sl, slb = SL_.next()
            for c in range(nch):
                ps, psb = PS.get()
                proj_tm(ps, psb, sl, slb, c, SLABW)
                act(ACP(VB[:, c, half * SLABW:(half + 1) * SLABW], ps[:, 0:SLABW]), (psb,), (sb_,))
        if smp:
            for n in range(nseq):
                dve(TS(VEX[:, n, :], VB[:, 0, :], MSEQ[:, n:n + 1], None, ALU.mult), (sb_, Kb), (sb_,))
        for half in range(2):
            sl, slb = SL_.next()
            for j in range(4):
                h = half * 4 + j
                ps, psb = PS.get()
                proj_fm(ps, psb, sl, slb, j * 64, 64)
                act(ACT(GS[0:64, h, :], ps[0:64, 0:N], AF.Silu), (psb,), (sb_,))
        _ckpt("B3_%d_%d" % (l, T.idx))
        for c in range(nch):
            cs = slice(c * 128, (c + 1) * 128)
            for g in range(2):
                psE, psEb = PS.get()
                psO, psOb = PS.get()
                th = []
                for j in (0, 2, 1, 3):
                    h = g * 4 + j
                    hp, po = h // 2, (h % 2) * 64
                    pp_ = psE if j % 2 == 0 else psO
                    th.append(MM(pp_[:, (j // 2) * 128:(j // 2 + 1) * 128], KB_[po:po + 64, hp, cs], QB[po:po + 64, hp, cs]))
                pe(SEQ(th), (sb_,), (psEb, psOb))
                dve(TT(SC[:, 0:4:2, :], psE[:, 0:256].rearrange("p (a b) -> p a b", a=2), DIN[:, g * 4:(g + 1) * 4:2, :], ALU.mult), (psEb, Tb), (sb_,))
                dve(TT(SC[:, 1:4:2, :], psO[:, 0:256].rearrange("p (a b) -> p a b", a=2), DIN[:, g * 4 + 1:(g + 1) * 4:2, :], ALU.mult), (psOb, Tb), (sb_,))
                pd, pdb = PSF[4 + g], PSFb[4 + g]
                th = []
                for j in range(4):
                    h = g * 4 + j
                    hp, po = h // 2, (h % 2) * 64
                    th.append(MM(pd[0:64, j * 128:(j + 1) * 128], VB[:, c, h * 64:(h + 1) * 64], SC[:, j, :], True, False))
                    for n in range(nseq):
                        th.append(MM(pd[0:64, j * 128 + n * SLc:j * 128 + (n + 1) * SLc], SBb[po:po + 64, n, hp, :],
                                     QS[po:po + 64, hp, c * 128 + n * SLc:c * 128 + (n + 1) * SLc], False, n == nseq - 1))
                pe(SEQ(th), (sb_, stb), (pdb,))
                act(ACP(OT[0:64, g * 4:(g + 1) * 4, cs], pd[0:64, 0:512].rearrange("p (a b) -> p a b", a=4)), (pdb,), (sb_,))
            if not smp:
                ps, psb = PS.get()
                th = []
                for h in range(8):
                    hp, po = h // 2, (h % 2) * 64
                    th.append(MM(ps[po:po + 64, hp * 64:(hp + 1) * 64], KT[:, c, h * 64:(h + 1) * 64], VB[:, c, h * 64:(h + 1) * 64]))
                pe(SEQ(th), (sb_,), (psb,))
                for hp in range(4):
                    dve(STT(SF[:, 0, hp, :], SF[:, 0, hp, :], DS[:, hp:hp + 1], ps[:, hp * 64:(hp + 1) * 64], ALU.mult, ALU.add), (psb, stb, Tb), (stb,))
            else:
                for h in range(8):
                    hp, po = h // 2, (h % 2) * 64
                    for hf in range(2):
                        ps, psb = PS.get()
                        psv = ps[po:po + 64, 0:512].rearrange("p (a b) -> p a b", a=8)
                        pe(MM(psv, KT[:, 0, h * 64:(h + 1) * 64], VEX[:, hf * 8:(hf + 1) * 8, h * 64:(h + 1) * 64]), (sb_,), (psb,))
                        sfv = SF[po:po + 64, hf * 8:(hf + 1) * 8, hp, :]
                        dve(STT(sfv, sfv, DS[po:po + 64, hp:hp + 1], psv, ALU.mult, ALU.add), (psb, stb, Tb), (stb,))
            dve(CP(SBb, SF), (stb,), (stb,))
        if smp or last_p:
            store(O["ret_" + sfx][l], SF, (stb,))
        _ckpt("B4_%d_%d" % (l, T.idx))
        MEAN = AR.alloc([N])
        VAR = AR.alloc([N])
        SQ = AR.alloc([N])
        OTb = AR.alloc([N], BF)
        SQb = AR.alloc([N], BF)
        for h in range(8):
            dve(CP(OTb[0:64], OT[0:64, h, :]), (sb_,), (sb_,))
            ps1, ps1b = PS.get()
            pe(MM(ps1[0:64, 0:N], onesb[0:64, 0:64], OTb[0:64]), (sb_, Kb), (ps1b,))
            act(ACT(SQb[0:64], OT[0:64, h, :], AF.Square), (sb_,), (sb_,))
            ps2, ps2b = PS.get()
            pe(MM(ps2[0:64, 0:N], onesb[0:64, 0:64], SQb[0:64]), (sb_, Kb), (ps2b,))
            act(ACT(MEAN[0:64], ps1[0:64, 0:N], AF.Copy, scale=1.0 / 64), (ps1b,), (sb_,))
            dve(TT(VAR[0:64], MEAN[0:64], MEAN[0:64], ALU.mult), (sb_,), (sb_,))
            dve(STT(VAR[0:64], ps2[0:64, 0:N], 1.0 / 64, VAR[0:64], ALU.mult, ALU.subtract), (ps2b, sb_), (sb_,))
            act(ACT(VAR[0:64], VAR[0:64], AF.Sqrt, bias=GN_EPS), (sb_,), (sb_,))
            dve(RCP(VAR[0:64], VAR[0:64]), (sb_,), (sb_,))
            dve(TT(SQ[0:64], OT[0:64, h, :], MEAN[0:64], ALU.subtract), (sb_,), (sb_,))
            dve(TT(SQ[0:64], SQ[0:64], VAR[0:64], ALU.mult), (sb_,), (sb_,))
            dve(STT(YB[0:64, h, :], SQ[0:64], P_("rgn", h, 64), GS[0:64, h, :], ALU.mult, ALU.mult), (sb_, Pb), (yb_,))
        AR.off = mark_phase
        S.barrier()
        _ckpt("%s%d_%d" % ("B", l, T.idx))

        CACT = AR.alloc([4, N])
        CAB = AR.alloc([4, N], BF)
        CQb = AR.alloc([4, N], BF)
        CKb = AR.alloc([4, N], BF)
        CVT = AR.alloc([4, N], BF)
        CV = AR.alloc([nch, 512], BF)
        CVEX = AR.alloc([nseq, 512], BF) if smp else None
        QT = AR.alloc([4, N], BF)
        KTt = AR.alloc([4, N], BF)
        KTM = AR.alloc([nch, 512], BF)
        HT = AR.alloc([4, N])
        IG = AR.alloc([N]); FG = AR.alloc([N]); MT = AR.alloc([N]); GG = AR.alloc([N]); ZR = AR.alloc([N])
        INTER = AR.alloc([N]); CPR = AR.alloc([N]); ENM = AR.alloc([N])
        BC1 = AR.alloc([N]); BC2 = AR.alloc([N]); DECB = AR.alloc([max(nseq * 4, 16)])
        SM = AR.alloc([128], BF)
        DEN = AR.alloc([128]); EB = AR.alloc([128])
        CC = AR.alloc([N])
        HTb = AR.alloc([N], BF)
        HSb = AR.alloc([N], BF)
        VH = AR.alloc([N], BF)
        VL = AR.alloc([N], BF)
        sc_ = Buf("scrC")
        for half in range(2):
            sl, slb = SL_.next()
            for j in range(2):
                h = half * 2 + j
                ps, psb = PS.get()
                proj_fm(ps, psb, sl, slb, j * 128, 128)
                act(ACP(CQH[:, h, :, 3:3 + SLn], v3(ps[:, 0:N])), (psb,), (stb,))
                conv(CQH, h, "mcw", "mcb", v3(CC), [stb, Pb], [sc_])
                act(ACT(CACT[:, h, :], CC, AF.Silu), (sc_,), (sc_,))
                dve(CP(CAB[:, h, :], CACT[:, h, :]), (sc_,), (sc_,))
                for qk, dst in ((0, CQb), (1, CKb)):
                    ps2, ps2b = PS.get()
                    pe(MM(ps2[:, 0:N], MWb[:, h * 2 + qk, :], CAB[:, h, :]), (sc_, Pb), (ps2b,))
                    act(ACP(dst[:, h, :], ps2[:, 0:N]), (ps2b,), (sc_,))
        if smp or last_p:
            store(O["mconv_" + sfx][l], CQH[:, :, :, SLn:SLn + 3], (stb,))
        else:
            dve(CP(CQH[:, :, :, 0:3], CQH[:, :, :, SLn:SLn + 3]), (stb,), (stb,))
        for half in range(2):
            sl, slb = SL_.next()
            for j in range(2):
                fc = half * 2 + j
                ps, psb = PS.get()
                proj_fm(ps, psb, sl, slb, j * 128, 128)
                act(ACP(CVT[:, fc, :], ps[:, 0:N]), (psb,), (sc_,))
            for c in range(nch):
                ps, psb = PS.get()
                proj_tm(ps, psb, sl, slb, c, SLABW)
                act(ACP(CV[:, c, half * SLABW:(half + 1) * SLABW], ps[:, 0:SLABW]), (psb,), (sc_,))
        if smp:
            for n in range(nseq):
                dve(TS(CVEX[:, n, :], CV[:, 0, :], MSEQ[:, n:n + 1], None, ALU.mult), (sc_, Kb), (sc_,))
        for gi, dstg in ((0, IG), (1, FG)):
            ps, psb = PS.get()
            srcs = [CQb[:, k, :] for k in range(4)] + [CKb[:, k, :] for k in range(4)] + [CVT[:, k, :] for k in range(4)]
            pe(SEQ([MM(ps[0:4, 0:N], WGb[:, k, gi * 4:(gi + 1) * 4], srcs[k], k == 0, k == 11) for k in range(12)]), (sc_, Pb), (psb,))
            act(ACT(dstg[0:4], ps[0:4, 0:N], AF.Identity, bias=P_("bg", gi, 4)), (psb, Pb), (sc_,))
        act(ACT(FG[0:4], FG[0:4], AF.Exp, scale=-1.0), (sc_,), (sc_,))
        act(ACT(FG[0:4], FG[0:4], AF.Ln, bias=1.0), (sc_,), (sc_,))
        dve(TS(FG[0:4], FG[0:4], -1.0, None, ALU.mult), (sc_,), (sc_,))
        dve(MSET(ZR[0:4], 0.0), (), (sc_,))
        for n in range(nseq):
            sn = slice(n * SLn, (n + 1) * SLn)
            dve(SCAN(MT[0:4, sn], FG[0:4, sn], IG[0:4, sn], MST[0:4, n:n + 1], ALU.add, ALU.max), (sc_, stb), (sc_,))
        LcS = 8 if smp else 128
        for k in range(N // LcS):
            sk = slice(k * LcS, (k + 1) * LcS)
            if smp:
                init = MST[0:4, k:k + 1]
            elif k == 0:
                init = MST[0:4, 0:1]
            else:
                init = MT[0:4, k * LcS - 1:k * LcS]
            dve(SCAN(GG[0:4, sk], FG[0:4, sk], ZR[0:4, sk], init, ALU.add, ALU.add), (sc_, stb), (sc_,))
        dve(TT(INTER[0:4], GG[0:4], MT[0:4], ALU.subtract), (sc_,), (sc_,))
        act(ACT(INTER[0:4], INTER[0:4], AF.Exp), (sc_,), (sc_,))
        dve(TT(CPR[0:4], IG[0:4], GG[0:4], ALU.subtract), (sc_,), (sc_,))
        act(ACT(CPR[0:4], CPR[0:4], AF.Exp), (sc_,), (sc_,))
        act(ACT(ENM[0:4], MT[0:4], AF.Exp, scale=-1.0), (sc_,), (sc_,))
        mst_new = MT[0:4, :].rearrange("p (n s) -> p n s", n=nseq)[:, :, SLn - 1]

        def bcast4(out_ps, out_b, h, vec, sl_):
            vh, vl = VH[0:4, sl_], VL[0:4, sl_]
            dve(CP(vh, vec), (sc_,), (sc_,))
            dve(TT(vl, vec, vh, ALU.subtract), (sc_,), (sc_,))
            pe(SEQ([MM(out_ps, self4[0:4, h, :], vh, True, False), MM(out_ps, self4[0:4, h, :], vl, False, True)]), (sc_, Kb), (out_b,))
        for h in range(4):
            psa, psab = PS.get()
            bcast4(psa[:, 0:N], psab, h, INTER[0:4], slice(0, N))
            dve(TT(QT[:, h, :], CQb[:, h, :], psa[:, 0:N], ALU.mult), (psab, sc_), (sc_,))
            lastcols = psa[:, 0:N].rearrange("p (n s) -> p n s", s=LcS)[:, :, LcS - 1]
            if smp:
                dve(CP(DECB.rearrange("p (n h) -> p n h", h=4)[:, :, h], lastcols), (psab,), (sc_,))
            else:
                dve(CP(DECB[:, h * 4:(h + 1) * 4], lastcols), (psab,), (sc_,))
            psc, pscb = PS.get()
            bcast4(psc[:, 0:N], pscb, h, CPR[0:4], slice(0, N))
            dve(STT(KTt[:, h, :], CKb[:, h, :], 128.0 ** -0.5, psc[:, 0:N], ALU.mult, ALU.mult), (pscb, sc_), (sc_,))
        for c in range(nch):
            pe(SEQ([TRN(PSB[:, h * 128:(h + 1) * 128], KTt[:, h, c * 128:(c + 1) * 128], identb) for h in range(4)]), (sc_, Kb), (PSBb,))
            dve(CP(KTM[:, c, :], PSB[:, 0:512]), (PSBb,), (sc_,))
        pn, pnb = PSF[4], PSFb[4]
        pdn, pdnb = PSF[5], PSFb[5]
        for c in range(nch):
            cs = slice(c * 128, (c + 1) * 128)
            for h in range(4):
                ps, psb = PS.get()
                pe(MM(ps[:, 0:128], KTt[:, h, cs], QT[:, h, cs]), (sc_,), (psb,))
                dve(TT(SM, ps[:, 0:128], CM, ALU.mult), (psb, Tb), (sc_,))
                th = [MM(pn[:, 0:128], CV[:, c, h * 128:(h + 1) * 128], SM, True, False)]
                for n in range(nseq):
                    th.append(MM(pn[:, n * SLc:(n + 1) * SLc], CBb[:, n, h, :], QT[:, h, c * 128 + n * SLc:c * 128 + (n + 1) * SLc], False, n == nseq - 1))
                pe(SEQ(th), (sc_, stb), (pnb,))
                th = [MM(pdn[:, 0:128], onesb, SM, True, False)]
                for n in range(nseq):
                    th.append(MM(pdn[:, n * SLc:(n + 1) * SLc], NBb[:, n * 4 + h, :], QT[:, h, c * 128 + n * SLc:c * 128 + (n + 1) * SLc], False, n == nseq - 1))
                pe(SEQ(th), (sc_, stb, Kb), (pdnb,))
                pse, pseb = PS.get()
                bcast4(pse[:, 0:128], pseb, h, ENM[0:4, cs], cs)
                act(ACP(EB, pse[:, 0:128]), (pseb,), (sc_,))
                act(ACT(DEN, pdn[:, 0:128], AF.Abs), (pdnb,), (sc_,))
                dve(TT(DEN, DEN, EB, ALU.max), (sc_,), (sc_,))
                dve(RCP(DEN, DEN), (sc_,), (sc_,))
                dve(TT(HT[:, h, cs], pn[:, 0:128], DEN, ALU.mult), (pnb, sc_), (sc_,))
                if not smp:
                    psu, psub = PS.get()
                    pe(MM(psu[:, 0:128], KTM[:, c, h * 128:(h + 1) * 128], CV[:, c, h * 128:(h + 1) * 128]), (sc_,), (psub,))
                    dsc = DECB[:, h * 4 + c:h * 4 + c + 1]
                    dve(TT(CF[:, 0, h, :], CF[:, 0, h, :], psu[:, 0:128], ALU.add), (psub, stb), (stb,))
                    dve(TS(CF[:, 0, h, :], CF[:, 0, h, :], dsc, None, ALU.mult), (stb, sc_), (stb,))
                    dve(RSUM(BC1[:, 0:1], KTt[:, h, cs]), (sc_,), (sc_,))
                    dve(TT(NF[:, 0, h:h + 1], NF[:, 0, h:h + 1], BC1[:, 0:1], ALU.add), (sc_, stb), (stb,))
                    dve(TS(NF[:, 0, h:h + 1], NF[:, 0, h:h + 1], dsc, None, ALU.mult), (stb, sc_), (stb,))
                    dve(CP(CBb[:, 0, h, :], CF[:, 0, h, :]), (stb,), (stb,))
                    dve(TS(NBb[:, h, :], onesf, NF[:, 0, h:h + 1], None, ALU.mult), (stb, Kb), (stb,))
                else:
                    for q4 in range(4):
                        psu, psub = PS.get()
                        psv = psu[:, 0:512].rearrange("p (a b) -> p a b", a=4)
                        pe(MM(psv, KTM[:, 0, h * 128:(h + 1) * 128], CVEX[:, q4 * 4:(q4 + 1) * 4, h * 128:(h + 1) * 128]), (sc_,), (psub,))
                        cfv = CF[:, q4 * 4:(q4 + 1) * 4, h, :]
                        dve(TT(cfv, cfv, psv, ALU.add), (psub, stb), (stb,))
                    dve(RSUM(BC1[:, 0:NSEQ], KTt[:, h, :].rearrange("p (n s) -> p n s", n=NSEQ)), (sc_,), (sc_,))
                    decv = DECB.rearrange("p (n h) -> p n h", h=4)[:, :, h]
                    dve(TT(NF[:, :, h], NF[:, :, h], BC1[:, 0:NSEQ], ALU.add), (sc_, stb), (stb,))
                    dve(TT(NF[:, :, h], NF[:, :, h], decv, ALU.mult), (sc_, stb), (stb,))
                    for n in range(nseq):
                        dve(TS(CF[:, n, h, :], CF[:, n, h, :], DECB[:, n * 4 + h:n * 4 + h + 1], None, ALU.mult), (stb, sc_), (stb,))
        dve(CP(MST[0:4, :], mst_new), (sc_,), (stb,))
        if smp or last_p:
            store(O["mc_" + sfx][l], CF, (stb,))
            store(O["mn_" + sfx][l], NF, (stb,))
            store(O["mm_" + sfx][l], MST[0:4, :], (stb,))
        for h in range(4):
            dve(CP(HTb, HT[:, h, :]), (sc_,), (sc_,))
            ps1, ps1b = PS.get()
            pe(MM(ps1[:, 0:N], onesb, HTb), (sc_, Kb), (ps1b,))
            act(ACT(HSb, HT[:, h, :], AF.Square), (sc_,), (sc_,))
            ps2, ps2b = PS.get()
            pe(MM(ps2[:, 0:N], onesb, HSb), (sc_, Kb), (ps2b,))
            act(ACT(BC2, ps1[:, 0:N], AF.Copy, scale=1.0 / 128), (ps1b,), (sc_,))
            dve(TT(BC1, BC2, BC2, ALU.mult), (sc_,), (sc_,))
            dve(STT(BC1, ps2[:, 0:N], 1.0 / 128, BC1, ALU.mult, ALU.subtract), (ps2b, sc_), (sc_,))
            act(ACT(BC1, BC1, AF.Sqrt, bias=GN_EPS), (sc_,), (sc_,))
            dve(RCP(BC1, BC1), (sc_,), (sc_,))
            dve(TT(HT[:, h, :], HT[:, h, :], BC2, ALU.subtract), (sc_,), (sc_,))
            dve(TT(HT[:, h, :], HT[:, h, :], BC1, ALU.mult), (sc_,), (sc_,))
            dve(TS(HT[:, h, :], HT[:, h, :], P_("mgn", h), None, ALU.mult), (sc_, Pb), (sc_,))
            dve(STT(HT[:, h, :], CACT[:, h, :], P_("mskip", h), HT[:, h, :], ALU.mult, ALU.add), (sc_, Pb), (sc_,))
            dve(CP(YC[:, h, :], HT[:, h, :]), (sc_,), (yb_,))
        AR.off = mark_phase
        S.barrier()
        _ckpt("%s%d_%d" % ("C", l, T.idx))

        scrD = (AR.alloc([N]), AR.alloc([N]), AR.alloc([N]), Buf("rotD"), AR.alloc([N], BF), AR.alloc([N], BF))
        QD = AR.alloc([6, N], BF)
        KF = AR.alloc([N])
        PEX = AR.alloc([128], BF)
        PM = AR.alloc([128], BF)
        VF = AR.alloc([SLABW])
        RC = AR.alloc([128])
        sd_ = Buf("scrD")
        kfb = Buf("kf")
        vfb = Buf("vf")
        if smp:
            KDs = AR.alloc([6, N], BF)
            VDs = AR.alloc([768], BF)
        for j3 in range(3):
            sl, slb = SL_.next()
            for j in range(2):
                fc = j3 * 2 + j
                ps, psb = PS.get()
                proj_fm(ps, psb, sl, slb, j * 128, 128)
                rotary(ps, psb, [(QD[:, fc, :], sd_)], scrD)
        _ckpt("Da_%d_%d" % (l, T.idx))
        for j3 in range(3):
            sl, slb = SL_.next()
            for j in range(2):
                fc = j3 * 2 + j
                g = fc // 2
                ps, psb = PS.get()
                proj_fm(ps, psb, sl, slb, j * 128, 128)
                rotary(ps, psb, [(KF, kfb)], scrD)
                if smp:
                    dve(CP(KDs[:, fc, :], KF), (kfb,), (sd_,))
                else:
                    for c in range(nch):
                        blk = T.idx * 4 + c
                        slot = blk % RG[g]
                        dve(CP(KD[g][:, fc % 2, slot * 128:(slot + 1) * 128], KF[:, c * 128:(c + 1) * 128]), (kfb,), (KDb[g][slot],))
                store(O["kT"][l][:, fc, tok0:tok0 + N], KF, (kfb,))
        _ckpt("Db_%d_%d" % (l, T.idx))
        for j3 in range(3):
            sl, slb = SL_.next()
            g = j3
            for c in range(nch):
                ps, psb = PS.get()
                proj_tm(ps, psb, sl, slb, c, SLABW)
                act(ACP(VF, ps[:, 0:SLABW]), (psb,), (vfb,))
                store(O["v"][l][tok0 + c * 128:tok0 + (c + 1) * 128, g * 256:(g + 1) * 256], VF, (vfb,))
                if smp:
                    dve(CP(VDs[:, g * 256:(g + 1) * 256], VF), (vfb,), (sd_,))
                else:
                    blk = T.idx * 4 + c
                    slot = blk % RG[g]
                    dve(CP(VD[g][:, slot, :], VF), (vfb,), (KDb[g][slot],))
        _ckpt("Dc_%d_%d" % (l, T.idx))
        pnum, pnumb = PSF[4], PSFb[4]
        pden, pdenb = PSF[5], PSFb[5]
        if not smp:
            for c in range(nch):
                qi = T.idx * 4 + c
                cs = slice(c * 128, (c + 1) * 128)
                for hh in range(4):
                    jobs = []
                    for g in range(3):
                        lo = max(0, qi - (1, 4, 15)[g])
                        for kj in range(lo, qi + 1):
                            if g == 0:
                                mk = 0 if kj == qi else 1
                            elif g == 1:
                                mk = 2 if kj == qi else (4 if kj == qi - 4 else 3)
                            else:
                                mk = 5 if kj == qi else 6
                            jobs.append((g, kj, mk))
                    for ji, (g, kj, mk) in enumerate(jobs):
                        h = g * 4 + hh
                        fcl, po = (h % 4) // 2, (h % 2) * 64
                        slot = kj % RG[g]
                        ps, psb = PS.get()
                        pe(MM(ps[:, 0:128], KD[g][po:po + 64, fcl, slot * 128:(slot + 1) * 128], QD[po:po + 64, h // 2, cs]), (sd_, KDb[g][slot]), (psb,))
                        act(ACT(PEX, ps[:, 0:128], AF.Exp, scale=0.125), (psb,), (sd_,))
                        dve(TT(PM, PEX, DMASK[:, mk, :], ALU.mult), (sd_, Kb), (sd_,))
                        st, sp_ = (ji == 0), (ji == len(jobs) - 1)
                        pe(MM(pnum[0:64, hh * 128:(hh + 1) * 128], VD[g][:, slot, hh * 64:(hh + 1) * 64], PM, st, sp_), (sd_, KDb[g][slot]), (pnumb,))
                        pe(MM(pden[0:64, hh * 128:(hh + 1) * 128], onesb[:, 0:64], PM, st, sp_), (sd_, Kb), (pdenb,))
                    dve(RCP(RC[0:64], pden[0:64, hh * 128:(hh + 1) * 128]), (pdenb,), (sd_,))
                    dve(TT(YD[0:64, hh, cs], pnum[0:64, hh * 128:(hh + 1) * 128], RC[0:64], ALU.mult), (pnumb, sd_), (yb_,))
        else:
            CT = [AR.alloc([512], BF) for _ in range(3)]
            CTb = [Buf() for _ in range(3)]
            KCT = AR.alloc([2, 128], BF)
            PX = AR.alloc([4, 8], BF)
            zr_rhs = DMASK[:, 0:4, :]
            pe(MM(pnum[0:64, 0:512].rearrange("p (a b) -> p a b", a=4), zerob[:, 0:64], zr_rhs, True, False), (Kb,), (pnumb,))
            pe(MM(pden[0:64, 0:512].rearrange("p (a b) -> p a b", a=4), zerob[:, 0:64], zr_rhs, True, False), (Kb,), (pdenb,))
            _ckpt("Dd_%d_%d" % (l, T.idx))
            for hh in range(4):
                for g in range(3):
                    h = g * 4 + hh
                    po = (h % 2) * 64
                    ps, psb = PS.get()
                    pe(MM(ps[:, 0:128], KDs[po:po + 64, h // 2, :], QD[po:po + 64, h // 2, :]), (sd_,), (psb,))
                    act(ACT(PEX, ps[:, 0:128], AF.Exp, scale=0.125), (psb,), (sd_,))
                    dve(TT(PM, PEX, DMASK[:, 7 + g, :], ALU.mult), (sd_, Kb), (sd_,))
                    pe(MM(pnum[0:64, hh * 128:(hh + 1) * 128], VDs[:, g * 256 + hh * 64:g * 256 + (hh + 1) * 64], PM, False, False), (sd_,), (pnumb,))
                    pe(MM(pden[0:64, hh * 128:(hh + 1) * 128], onesb[:, 0:64], PM, False, False), (sd_, Kb), (pdenb,))
            _ckpt("D1_%d_%d" % (l, T.idx))
            ti = 0
            for n in range(NSEQ):
                for g, (cname, dil, ntile) in enumerate((("c1", 1, 1), ("c4", 4, 4), ("c16", 16, 8))):
                    for r in range(ntile):
                        ct, ctb = CT[ti % 3], CTb[ti % 3]
                        ti += 1
                        src = I[cname][l][n].rearrange("(i d) c -> d i c", d=dil)[r]
                        dma("pool", ct, src, (), (ctb,))
                        pe(SEQ([TRN(PSB[:, 0:128], ct[:, 0:128], identb), TRN(PSB[:, 128:256], ct[:, 128:256], identb)]), (ctb, Kb), (PSBb,))
                        dve(CP(KCT, PSB[:, 0:256].rearrange("p (a b) -> p a b", a=2)), (PSBb,), (sd_,))
                        nq = 8
                        qsl = slice(n * 8, n * 8 + 8)
                        psE, psEb = PS.get()
                        psO, psOb = PS.get()
                        th = []
                        for hh in (0, 2, 1, 3):
                            h = g * 4 + hh
                            po = (h % 2) * 64
                            pp_ = psE if hh % 2 == 0 else psO
                            th.append(MM(pp_[:, (hh // 2) * nq:(hh // 2 + 1) * nq], KCT[po:po + 64, hh // 2, :], QD[po:po + 64, h // 2, qsl]))
                        pe(SEQ(th), (sd_,), (psEb, psOb))
                        pxv = PX[:, :, 0:nq]
                        act(ACT(PX[:, 0:4:2, 0:nq], psE[:, 0:2 * nq].rearrange("p (a b) -> p a b", a=2), AF.Exp, scale=0.125), (psEb,), (sd_,))
                        act(ACT(PX[:, 1:4:2, 0:nq], psO[:, 0:2 * nq].rearrange("p (a b) -> p a b", a=2), AF.Exp, scale=0.125), (psOb,), (sd_,))
                        cmk = CMASK0 if g == 0 else (CMASK1[:, r] if g == 1 else CMASK2[:, r])
                        dve(TT(pxv, pxv, cmk, ALU.mult), (sd_, Kb), (sd_,))
                        lastt = (n == NSEQ - 1 and g == 2 and r == ntile - 1)
                        th = []
                        for hh in range(4):
                            th.append(MM(pnum[0:64, hh * 128:(hh + 1) * 128][:, qsl], ct[:, 256 + hh * 64:256 + (hh + 1) * 64], PX[:, hh, 0:nq], False, lastt and hh == 3))
                            th.append(MM(pden[0:64, hh * 128:(hh + 1) * 128][:, qsl], onesb[:, 0:64], PX[:, hh, 0:nq], False, lastt and hh == 3))
                        pe(SEQ(th), (sd_, ctb, Kb), (pnumb, pdenb))
                        if n == 0 and r == 0:
                            _ckpt("Dg%d_%d_%d" % (g, l, T.idx))
            for hh in range(4):
                dve(RCP(RC[0:64], pden[0:64, hh * 128:(hh + 1) * 128]), (pdenb,), (sd_,))
                dve(TT(YD[0:64, hh, :], pnum[0:64, hh * 128:(hh + 1) * 128], RC[0:64], ALU.mult), (pnumb, sd_), (yb_,))
        ZS = AR.alloc([N])
        for half in range(2):
            sl, slb = SL_.next()
            for j in range(2):
                h = half * 2 + j
                ps, psb = PS.get()
                proj_fm(ps, psb, sl, slb, j * 128, 128)
                act(ACT(ZS, ps[:, 0:N], AF.Sigmoid), (psb,), (sd_,))
                dve(TT(YC[:, h, :], YC[:, h, :], ZS, ALU.mult), (sd_, yb_), (yb_,))
        AR.off = mark_phase
        S.barrier()
        _ckpt("%s%d_%d" % ("D", l, T.idx))

        if _DBG and l == 0 and T.idx == 0:
            ev = dma("pool", DBG[:, 0:4, :], YA, (yb_,), ()); S.store_evs.append(ev)
            ev = dma("pool", DBG[0:64, 4:12, :], YB[0:64], (yb_,), ()); S.store_evs.append(ev)
            ev = dma("pool", DBG[:, 12:16, :], YC, (yb_,), ()); S.store_evs.append(ev)
            ev = dma("pool", DBG[0:64, 16:20, :], YD[0:64], (yb_,), ()); S.store_evs.append(ev)
        MG = AR.alloc([16, N], BF)
        ACC = AR.alloc([2, N])
        TBUF = AR.alloc([16, N])
        SIG = AR.alloc([N]); TMP = AR.alloc([N]); XR = AR.alloc([2, N])
        MEAN = AR.alloc([N]); RSTD = AR.alloc([N])
        mgb = Buf("mg"); tb_ = Buf("tbuf"); s1 = Buf("scr1"); xrb = [Buf(), Buf()]
        ysrc = [(YA, 4, 128), (YB, 8, 64), (YC, 4, 128), (YD, 4, 64)]
        SIGJ = AR.alloc([2, N])
        for og in range(D // SLABW):
            for m in range(4):
                ysb, kcs, prow = ysrc[m]
                slg, slgb = SL_.next()
                for j in range(2):
                    psg, psgb = PS.get()
                    proj_fm(psg, psgb, slg, slgb, j * 128, 128)
                    act(ACT(SIGJ[:, j, :], psg[:, 0:N], AF.Sigmoid), (psgb,), (s1,))
                slw, slwb = SL_.next()
                for j in range(2):
                    oc = og * 2 + j
                    psw_, pswb_ = PS.get()
                    pe(SEQ([MM(psw_[:, 0:N], slw[0:prow, k, j * 128:(j + 1) * 128], ysb[0:prow, k, :], k == 0, k == kcs - 1) for k in range(kcs)]), (slwb, yb_), (pswb_,))
                    if m == 0:
                        dve(TT(ACC[:, j, :], SIGJ[:, j, :], psw_[:, 0:N], ALU.mult), (s1, pswb_), (s1,))
                    else:
                        dve(TT(TMP, SIGJ[:, j, :], psw_[:, 0:N], ALU.mult), (s1, pswb_), (s1,))
                        dve(TT(ACC[:, j, :], ACC[:, j, :], TMP, ALU.add), (s1,), (s1,))
                    if m == 3:
                        dve(CP(MG[:, oc, :], ACC[:, j, :]), (s1,), (mgb,))
        if _DBG and l == 0 and T.idx == 0:
            ev = dma("pool", DBG[:, 20:36, :], MG, (mgb,), ()); S.store_evs.append(ev)

        def layer_norm(gname, bname, out_fn, TBUF, tb_, SIG, TMP, MEAN, RSTD, s1):
            p1, p1b = PSF[4], PSFb[4]
            p2, p2b = PSF[5], PSFb[5]
            SGb = SIG.bitcast(BF)
            Sq_b, T_b = SGb[:, 0:N], SGb[:, N:2 * N]
            for oc in range(16):
                act(ACT(Sq_b, TBUF[:, oc, :], AF.Square), (tb_,), (s1,))
                dve(CP(T_b, TBUF[:, oc, :]), (tb_,), (s1,))
                pe(MM(p1[:, 0:N], onesb, T_b, oc == 0, oc == 15), (s1, Kb), (p1b,))
                pe(MM(p2[:, 0:N], onesb, Sq_b, oc == 0, oc == 15), (s1, Kb), (p2b,))
            act(ACT(MEAN, p1[:, 0:N], AF.Copy, scale=1.0 / D), (p1b,), (s1,))
            dve(TT(RSTD, MEAN, MEAN, ALU.mult), (s1,), (s1,))
            dve(STT(RSTD, p2[:, 0:N], 1.0 / D, RSTD, ALU.mult, ALU.subtract), (p2b, s1), (s1,))
            act(ACT(RSTD, RSTD, AF.Sqrt, bias=LN_EPS), (s1,), (s1,))
            dve(RCP(RSTD, RSTD), (s1,), (s1,))
            for oc in range(16):
                dve(TT(TMP, TBUF[:, oc, :], MEAN, ALU.subtract), (tb_, s1), (s1,))
                dve(TT(TMP, TMP, RSTD, ALU.mult), (s1,), (s1,))
                out_fn(oc, TMP, P_(gname, oc), P_(bname, oc))

        for og in range(D // SLABW):
            sl, slb = SL_.next()
            for j in range(2):
                oc = og * 2 + j
                ps, psb = PS.get()
                pe(SEQ([MM(ps[:, 0:N], sl[:, k, j * 128:(j + 1) * 128], MG[:, k, :], k == 0, k == 15) for k in range(16)]), (slb, mgb), (psb,))
                xr, xb = XR[:, oc % 2, :], xrb[oc % 2]
                dma("sp", xr, xsrc[:, oc, tok0:tok0 + N], (XL1b[T.idx],), (xb,))
                dve(STT(TBUF[:, oc, :], xr, ALPHA, ps[:, 0:N], ALU.mult, ALU.add), (xb, psb), (tb_,))

        SIG1, s1_1 = SIG, s1

        def out1(oc, xn, g, b):
            act(ACT(SIG1, xn, AF.Identity, bias=b, scale=g), (s1_1, Pb), (s1_1,))
            dve(CP(XTB[:, oc, :], SIG1), (s1_1,), (xtb,))
            dma("sp", X1S[:, oc, 0:N], SIG1, (s1_1,), (X1Sb[oc],))
            if _DBG and l == 0 and T.idx == 0:
                ev = dma("sp", DBG[:, 36 + oc, :], SIG1, (s1_1,), ()); S.store_evs.append(ev)
        layer_norm("ln1g", "ln1b", out1, TBUF, tb_, SIG, TMP, MEAN, RSTD, s1)
        AR.off = mark_phase
        S.barrier()
        _ckpt("%s%d_%d" % ("B12", l, T.idx))

        HID = AR.alloc([64, N], BF)
        RL = AR.alloc([2, N]); SIG2 = AR.alloc([N]); TMP2 = AR.alloc([N]); XR2 = AR.alloc([2, N])
        hb = Buf("hid"); tb2 = Buf("tbuf2"); s2 = Buf("scr2"); xrb2 = [Buf(), Buf()]; rlb = [Buf(), Buf()]
        for hg in range(DFF // SLABW):
            sl, slb = SL_.next()
            for j in range(2):
                hc = hg * 2 + j
                ps, psb = PS.get()
                proj_fm(ps, psb, sl, slb, j * 128, 128)
                act(ACT(RL[:, j, :], ps[:, 0:N], AF.Relu, bias=P_("bup", hc)), (psb, Pb), (rlb[j],))
                dve(TT(HID[:, hc, :], RL[:, j, :], RL[:, j, :], ALU.mult), (rlb[j],), (hb,))
        S.barrier()
        _cur = AR.off
        AR.off = tile_off
        TBUF2 = AR.alloc([16, N])
        assert AR.off <= mark_phase
        AR.off = _cur
        MEAN2, RSTD2 = RL[:, 0, :], RL[:, 1, :]
        for og in range(D // SLABW):
            pacc = [(PSF[0], PSFb[0]), (PSF[1], PSFb[1])]
            for kq in range(4):
                sl, slb = SL_.next()
                for j in range(2):
                    pa, pab = pacc[j]
                    pe(SEQ([MM(pa[:, 0:N], sl[:, k, j * 128:(j + 1) * 128], HID[:, kq * 16 + k, :], (kq == 0 and k == 0), (kq == 3 and k == 15)) for k in range(16)]), (slb, hb), (pab,))
            for j in range(2):
                oc = og * 2 + j
                pa, pab = pacc[j]
                xr, xb = XR2[:, j, :], xrb2[j]
                dma("sp", xr, X1S[:, oc, 0:N], (X1Sb[oc],), (xb,))
                act(ACT(TMP2, pa[:, 0:N], AF.Identity, bias=P_("bdown", oc)), (pab, Pb), (s2,))
                dve(STT(TBUF2[:, oc, :], xr, ALPHA, TMP2, ALU.mult, ALU.add), (xb, s2), (tb2,))

        def out2(oc, xn, g, b):
            act(ACT(SIG2, xn, AF.Identity, bias=b, scale=g), (s2, Pb), (s2,))
            ev = dma("sp", xdst[:, oc, tok0:tok0 + N], SIG2, (s2,), (XL1b[T.idx],) if l == 0 else ())
            if l == 1:
                S.store_evs.append(ev)
            if _DBG and l == 0 and T.idx == 0:
                ev = dma("sp", DBG[:, 52 + oc, :], SIG2, (s2,), ()); S.store_evs.append(ev)
        layer_norm("ln2g", "ln2b", out2, TBUF2, tb2, SIG2, TMP2, MEAN2, RSTD2, s2)
        AR.off = mark_phase
        S.barrier()
        _ckpt("%s%d_%d" % ("B34", l, T.idx))

    try:
        for l in range(NL):
            load_params(l)
            for T in tiles:
                if T.idx == 0 or T.sample:
                    alloc_phase(l, T.sample)
                do_tile(l, T)
    except _Stop:
        pass

    S.op("sp", None, extra=list(S.store_evs) + list(S.slot_ev.values()))
    print("arena peak words", AR.peak, "of", ARW)

    with ExitStack() as es2:
        csem = {}
        for e in Sched.ENG:
            nsig = sum(1 for o in S.ops[e] if o["sig"])
            csem[e] = [es2.enter_context(nc.semaphore("c_%s_%d" % (e, k))) for k in range(nsig // EPOCH + 1)]
        dsem = {}
        for q in ("pool", "sp"):
            for k in range(NSLOT):
                dsem[(q, k)] = es2.enter_context(nc.semaphore("d_%s_%d" % (q, k)))
        block = es2.enter_context(nc.Block())

        @block.tensor
        def _(h):
            S.emit("pe", h, csem, dsem)

        @block.scalar
        def _(h):
            S.emit("act", h, csem, dsem)

        @block.vector
        def _(h):
            S.emit("dve", h, csem, dsem)

        @block.gpsimd
        def _(h):
            S.emit("pool", h, csem, dsem)

        @block.sync
        def _(h):
            S.emit("sp", h, csem, dsem)
    es.close()
    return nc, cst, PC, NPP


def _fm(v):
    v = np.asarray(v, np.float32)
    return v


def kernel(**inp):
    f32 = np.float32
    _cores = inp.pop('_cores', list(range(8)))
    _raw = inp.pop('_raw', False)
    nc, cst, PC, NPP = build_program()
    g = lambda k: np.asarray(inp[k], f32)
    pp = np.zeros((NL, 128, NPP), f32)

    def put(name, arr):
        n = arr.shape[2]
        pp[:, :arr.shape[1], PC[name]:PC[name] + n] = arr

    def chan(v, nchunk):
        return v.reshape(NL, nchunk, 128).transpose(0, 2, 1)

    lcw = g("lru_conv_w")
    put("lcw", lcw.reshape(NL, 4, 4, 128).transpose(0, 3, 2, 1).reshape(NL, 128, 16))
    put("lcb", chan(g("lru_conv_b"), 4)); put("lba", chan(g("lru_ba"), 4)); put("lbx", chan(g("lru_bx"), 4))
    put("llam", chan(g("lru_lambda"), 4))
    mcw = g("mlstm_conv_w")
    put("mcw", mcw.reshape(NL, 4, 4, 128).transpose(0, 3, 2, 1).reshape(NL, 128, 16))
    put("mcb", chan(g("mlstm_conv_b"), 4)); put("mskip", chan(g("mlstm_skip"), 4)); put("mgn", chan(g("mlstm_gn_g"), 4))
    put("rgn", g("ret_gn_g").reshape(NL, 8, 64).transpose(0, 2, 1))
    put("ln1g", chan(g("ln1_g"), 16)); put("ln1b", chan(g("ln1_b"), 16)); put("ln2g", chan(g("ln2_g"), 16)); put("ln2b", chan(g("ln2_b"), 16))
    put("bdown", chan(g("b_down"), 16)); put("bup", chan(g("b_up"), 64))
    put("bg", g("mlstm_b_gates").reshape(NL, 2, 4).transpose(0, 2, 1))
    lw = np.zeros((NL, 128, 8, 128), f32)
    for fc in range(4):
        for bb in range(2):
            blk = fc * 2 + bb
            lw[:, bb * 64:(bb + 1) * 64, fc * 2 + 0, bb * 64:(bb + 1) * 64] = g("lru_wa")[:, blk]
            lw[:, bb * 64:(bb + 1) * 64, fc * 2 + 1, bb * 64:(bb + 1) * 64] = g("lru_wx")[:, blk]
    mw = np.zeros((NL, 128, 8, 128), f32)
    for h in range(4):
        mw[:, :, h * 2 + 0, :] = g("mlstm_wq")[:, h]
        mw[:, :, h * 2 + 1, :] = g("mlstm_wk")[:, h]
    wg = g("mlstm_w_gates").reshape(NL, 12, 128, 8).transpose(0, 2, 1, 3)
    common = {"pp": pp, "lw": lw, "mw": mw, "wg": np.ascontiguousarray(wg)}
    for nm in ("w_in", "w_br_a", "w_br_b", "w_br_c", "w_br_d", "w_out", "w_up", "w_down"):
        common[nm] = g(nm)
    for k, v in cst.items():
        common["k_" + k] = np.ascontiguousarray(v)
    xp = g("x_prompt"); xs = g("x_sample")
    in_maps = []
    for c in _cores:
        b = c // 2
        ns = slice(c * NSEQ, (c + 1) * NSEQ)
        xtok = np.concatenate([xp[b], xs[ns].reshape(N_S, D)], axis=0)
        m = dict(common)
        m["xT"] = np.ascontiguousarray(xtok.reshape(NTOK, 16, 128).transpose(2, 1, 0))
        for src, dst in (("state_lru_conv", "s_lconv"), ("state_mlstm_conv", "s_mconv")):
            m[dst] = np.ascontiguousarray(g(src)[:, ns].reshape(NL, NSEQ, 3, 4, 128).transpose(0, 4, 3, 1, 2))
        m["s_lh"] = np.ascontiguousarray(g("state_lru_h")[:, ns].reshape(NL, NSEQ, 4, 128).transpose(0, 3, 2, 1))
        m["s_ret"] = np.ascontiguousarray(g("state_ret")[:, ns].reshape(NL, NSEQ, 4, 2, 64, 64).transpose(0, 3, 4, 1, 2, 5).reshape(NL, 128, NSEQ, 4, 64))
        m["s_mc"] = np.ascontiguousarray(g("state_mlstm_c")[:, ns].transpose(0, 3, 1, 2, 4))
        m["s_mn"] = np.ascontiguousarray(g("state_mlstm_n")[:, ns].transpose(0, 3, 1, 2))
        m["s_mm"] = np.ascontiguousarray(g("state_mlstm_m")[:, ns].transpose(0, 2, 1))
        m["c1"] = np.ascontiguousarray(g("cache_dil1_kv")[:, ns].reshape(NL, NSEQ, 128, 512))
        m["c4"] = np.ascontiguousarray(g("cache_dil4_kv")[:, ns].reshape(NL, NSEQ, 512, 512))
        m["c16"] = np.ascontiguousarray(g("cache_dil16_kv")[:, ns].reshape(NL, NSEQ, 2048, 512))
        in_maps.append(m)
    res = run_bass_kernel_spmd(nc, in_maps, core_ids=list(range(len(_cores))))
    R = res.results
    if _raw:
        return R
    def tokmajor(yT):
        return yT.transpose(2, 1, 0).reshape(yT.shape[2], D)
    y_p = np.stack([tokmajor(R[2 * b]["yT"][:, :, :T_P]) for b in range(4)])
    y_s = np.concatenate([tokmajor(R[c]["yT"][:, :, T_P:]).reshape(NSEQ, 8, D) for c in range(8)])

    def gather(name, fn_p, fn_s):
        p = np.stack([fn_p(R[2 * b][name + "_p"]) for b in range(4)], axis=1)
        s = np.concatenate([fn_s(R[c][name + "_s"]) for c in range(8)], axis=1)
        return p, s
    conv_f = lambda a: a.transpose(0, 3, 4, 2, 1).reshape(NL, a.shape[3], 3, 512)
    p_lconv, s_lconv = gather("o_lconv", lambda a: conv_f(a)[:, 0], conv_f)
    lh_f = lambda a: a.transpose(0, 3, 2, 1).reshape(NL, a.shape[3], 512)
    p_lh, s_lh = gather("o_lh", lambda a: lh_f(a)[:, 0], lh_f)
    ret_f = lambda a: a.reshape(NL, 2, 64, a.shape[2], 4, 64).transpose(0, 3, 4, 1, 2, 5).reshape(NL, a.shape[2], 8, 64, 64)
    p_ret, s_ret = gather("o_ret", lambda a: ret_f(a)[:, 0], ret_f)
    p_mconv, s_mconv = gather("o_mconv", lambda a: conv_f(a)[:, 0], conv_f)
    mc_f = lambda a: a.transpose(0, 2, 3, 1, 4)
    p_mc, s_mc = gather("o_mc", lambda a: mc_f(a)[:, 0], mc_f)
    mn_f = lambda a: a.transpose(0, 2, 3, 1)
    p_mn, s_mn = gather("o_mn", lambda a: mn_f(a)[:, 0], mn_f)
    mm_f = lambda a: a.transpose(0, 2, 1)
    p_mm, s_mm = gather("o_mm", lambda a: mm_f(a)[:, 0], mm_f)

    def kv(c, g_, sl):
        kT = R[c]["o_kT"][:, :, g_ * 2:(g_ + 1) * 2, sl]
        k = kT.transpose(0, 3, 2, 1).reshape(NL, kT.shape[3], 256)
        v = R[c]["o_v"][:, sl, g_ * 256:(g_ + 1) * 256]
        return np.stack([k, v], axis=2).reshape(NL, k.shape[1], 2, 4, 64)
    pkv, skv = [], []
    for g_, win in enumerate((128, 512, 2048)):
        pkv.append(np.stack([kv(2 * b, g_, slice(T_P - win, T_P)) for b in range(4)], axis=1))
        skv.append(np.concatenate([kv(c, g_, slice(T_P, NTOK)).reshape(NL, NSEQ, 8, 2, 4, 64) for c in range(8)], axis=1))
    outs = (y_p, y_s, p_lconv, p_lh, p_ret, p_mconv, p_mc, p_mn, p_mm, pkv[0], pkv[1], pkv[2],
            s_lconv, s_lh, s_ret, s_mconv, s_mc, s_mn, s_mm, skv[0], skv[1], skv[2])
    return tuple(np.ascontiguousarray(o, dtype=np.float32) for o in outs)
```
